# Optimizing a Trainium2 kernel written in Bass

```python
import math
import jax, jax.numpy as jnp
from jax import lax
import numpy as np

D_MODEL = 1024
BATCH = 8
SEQ = 4096
DEPTH = 2

HEAD_DIM = 64
N_HEADS_A = D_MODEL // HEAD_DIM
N_HEADS_B = D_MODEL // HEAD_DIM
N_KV_B = 4
GROUP_B = N_HEADS_B // N_KV_B
D_FF = 2816
DILATED_PATTERNS = ((128, 1), (512, 4), (2048, 16))
WINDOW_B = 128
BLOCK = 128
N_A_LAYERS = DEPTH // 2
N_B_LAYERS = DEPTH - N_A_LAYERS
ALPHA = (2.0 * DEPTH) ** 0.25
BETA = (8.0 * DEPTH) ** -0.25
LN_EPS = 1e-5

kernel_name = "yoco_dilated_swa_sink_hybrid"


def alibi_slopes(n):
    return np.array([2.0 ** (-8.0 * (h + 1) / n) for h in range(n)], dtype=np.float32)


def layer_norm(x, g, b):
    xf = x.astype(jnp.float32)
    mu = xf.mean(-1, keepdims=True)
    var = jnp.mean(jnp.square(xf - mu), -1, keepdims=True)
    y = (xf - mu) * lax.rsqrt(var + LN_EPS) * g.astype(jnp.float32) + b.astype(jnp.float32)
    return y.astype(x.dtype)


def swiglu(x, w_in, w_out):
    gate, up = jnp.split(x @ w_in, 2, axis=-1)
    return (jax.nn.silu(gate) * up) @ w_out


def banded_attention(q, k, v, slopes, max_dist, dist_scale, sinks=None):
    b, L, hk, g, dh = q.shape
    P = BLOCK
    nb = -(-L // P)
    pad = nb * P - L
    if pad:
        q = jnp.pad(q, ((0, 0), (0, pad), (0, 0), (0, 0), (0, 0)))
        k = jnp.pad(k, ((0, 0), (0, pad), (0, 0), (0, 0)))
        v = jnp.pad(v, ((0, 0), (0, pad), (0, 0), (0, 0)))
    qb = q.reshape(b, nb, P, hk, g, dh)

    def with_prev(t):
        t = t.reshape(b, nb, P, hk, dh)
        prev = jnp.concatenate([jnp.zeros_like(t[:, :1]), t[:, :-1]], axis=1)
        return jnp.concatenate([prev, t], axis=2)

    kc, vc = with_prev(k), with_prev(v)
    s = jnp.einsum('bnqhgd,bnkhd->bnhgqk', qb.astype(jnp.float32), kc.astype(jnp.float32)) * (dh ** -0.5)
    qi = np.arange(P)[:, None]
    kj = np.arange(2 * P)[None, :]
    dist = P + qi - kj
    kpos = (np.arange(nb)[:, None, None] - 1) * P + kj[None]
    valid = (dist >= 0) & (dist <= max_dist) & (kpos >= 0)
    bias = -(slopes.astype(jnp.float32)[:, :, None, None]
             * jnp.asarray((dist * dist_scale).astype(np.float32)))
    s = jnp.where(jnp.asarray(valid)[None, :, None, None], s + bias, -jnp.inf)
    m = s.max(-1, keepdims=True)
    if sinks is not None:
        sink = sinks.astype(jnp.float32)[:, :, None, None]
        m = jnp.maximum(m, sink)
    p = jnp.exp(s - m)
    den = p.sum(-1, keepdims=True)
    if sinks is not None:
        den = den + jnp.exp(sink - m)
    o = jnp.einsum('bnhgqk,bnkhd->bnqhgd', p / den, vc.astype(jnp.float32))
    lse = jnp.moveaxis((m + jnp.log(den))[..., 0], -1, 2)
    o = o.reshape(b, nb * P, hk, g, dh)[:, :L].astype(q.dtype)
    lse = lse.reshape(b, nb * P, hk, g)[:, :L]
    return o, lse


def to_strided(t, d):
    b, S = t.shape[:2]
    t = t.reshape(b, S // d, d, *t.shape[2:])
    t = jnp.moveaxis(t, 2, 1)
    return t.reshape(b * d, S // d, *t.shape[3:])


def from_strided(t, b, d):
    t = t.reshape(b, d, *t.shape[1:])
    t = jnp.moveaxis(t, 1, 2)
    return t.reshape(b, t.shape[1] * d, *t.shape[3:])


def dilated_mixer(h, w_qkv, w_o):
    b, S, _ = h.shape
    q, k, v = jnp.split(h @ w_qkv, 3, axis=-1)
    q = q.reshape(b, S, N_HEADS_A, 1, HEAD_DIM)
    k = k.reshape(b, S, N_HEADS_A, HEAD_DIM)
    v = v.reshape(b, S, N_HEADS_A, HEAD_DIM)
    slopes = jnp.asarray(alibi_slopes(N_HEADS_A)).reshape(N_HEADS_A, 1)
    outs, lses = [], []
    for window, d in DILATED_PATTERNS:
        o, lse = banded_attention(to_strided(q, d), to_strided(k, d), to_strided(v, d),
                                  slopes, window // d, d)
        outs.append(from_strided(o, b, d))
        lses.append(from_strided(lse, b, d))
    wts = jax.nn.softmax(jnp.stack(lses, 0), axis=0)
    out = jnp.sum(wts[..., None] * jnp.stack(outs, 0).astype(jnp.float32), axis=0)
    return out.astype(h.dtype).reshape(b, S, D_MODEL) @ w_o


def swa_sink_mixer(h, k_sh, v_sh, w_q, sinks, w_o):
    b, S, _ = h.shape
    q = (h @ w_q).reshape(b, S, N_KV_B, GROUP_B, HEAD_DIM)
    slopes = jnp.asarray(alibi_slopes(N_HEADS_B)).reshape(N_KV_B, GROUP_B)
    o, _ = banded_attention(q, k_sh, v_sh, slopes, WINDOW_B - 1, 1,
                            sinks.reshape(N_KV_B, GROUP_B))
    return o.reshape(b, S, D_MODEL) @ w_o


def setup_inputs(seed: int = 0) -> dict:
    key = jax.random.key(seed)
    ks = jax.random.split(key, 16)
    f32 = jnp.float32
    nrm = lambda k, shape, fan_in, scale=1.0: jax.random.normal(k, shape, f32) * (fan_in ** -0.5) * scale
    return {
        "x": jax.random.normal(ks[0], (BATCH, SEQ, D_MODEL), f32),
        "ffn1_w_in": nrm(ks[1], (DEPTH, D_MODEL, 2 * D_FF), D_MODEL),
        "ffn1_w_out": nrm(ks[2], (DEPTH, D_FF, D_MODEL), D_FF, BETA),
        "ffn2_w_in": nrm(ks[3], (DEPTH, D_MODEL, 2 * D_FF), D_MODEL),
        "ffn2_w_out": nrm(ks[4], (DEPTH, D_FF, D_MODEL), D_FF, BETA),
        "ln_g": 1.0 + 0.02 * jax.random.normal(ks[5], (DEPTH, 3, D_MODEL), f32),
        "ln_b": 0.02 * jax.random.normal(ks[6], (DEPTH, 3, D_MODEL), f32),
        "a_w_qkv": nrm(ks[7], (N_A_LAYERS, D_MODEL, 3 * N_HEADS_A * HEAD_DIM), D_MODEL),
        "a_w_o": nrm(ks[8], (N_A_LAYERS, N_HEADS_A * HEAD_DIM, D_MODEL), D_MODEL, BETA),
        "kv_w": nrm(ks[9], (D_MODEL, 2 * N_KV_B * HEAD_DIM), D_MODEL),
        "b_w_q": nrm(ks[10], (N_B_LAYERS, D_MODEL, N_HEADS_B * HEAD_DIM), D_MODEL),
        "b_sinks": 0.5 * jax.random.normal(ks[11], (N_B_LAYERS, N_HEADS_B), f32),
        "b_w_o": nrm(ks[12], (N_B_LAYERS, N_HEADS_B * HEAD_DIM, D_MODEL), D_MODEL, BETA),
    }


def reference(x, ffn1_w_in, ffn1_w_out, ffn2_w_in, ffn2_w_out, ln_g, ln_b,
              a_w_qkv, a_w_o, kv_w, b_w_q, b_sinks, b_w_o):
    b, S, _ = x.shape
    k_sh = v_sh = None
    for i in range(DEPTH):
        x = layer_norm(ALPHA * x + 0.5 * swiglu(x, ffn1_w_in[i], ffn1_w_out[i]), ln_g[i, 0], ln_b[i, 0])
        if i < N_A_LAYERS:
            mix = dilated_mixer(x, a_w_qkv[i], a_w_o[i])
        else:
            j = i - N_A_LAYERS
            mix = swa_sink_mixer(x, k_sh, v_sh, b_w_q[j], b_sinks[j], b_w_o[j])
        x = layer_norm(ALPHA * x + mix, ln_g[i, 1], ln_b[i, 1])
        x = layer_norm(ALPHA * x + 0.5 * swiglu(x, ffn2_w_in[i], ffn2_w_out[i]), ln_g[i, 2], ln_b[i, 2])
        if i == N_A_LAYERS - 1:
            k_flat, v_flat = jnp.split(x @ kv_w, 2, axis=-1)
            k_sh = k_flat.reshape(b, S, N_KV_B, HEAD_DIM)
            v_sh = v_flat.reshape(b, S, N_KV_B, HEAD_DIM)
    return x
```

```python
import numpy as np
from contextlib import ExitStack
import concourse.bass as bass
import concourse.mybir as mybir
from concourse.bass_utils import run_bass_kernel_spmd

F32 = mybir.dt.float32
BF16 = mybir.dt.bfloat16
ALU = mybir.AluOpType
AF = mybir.ActivationFunctionType

S = 4096
D = 1024
DFF = 2816
NFC = 22
T = 512
NT = S // T
NBLK = T // 128
NCORES = 8
ALPHA = 4.0 ** 0.25
LN_EPS = 1e-5
PATTERNS = ((128, 1), (512, 4), (2048, 16))
SELF_SYNC = True
WDEPTH = 8


class Evt:
    __slots__ = ("sem", "val", "eng", "stream")

    def __init__(self, sem, val, eng, stream=False):
        self.sem = sem
        self.val = val
        self.eng = eng
        self.stream = stream


class Buf:
    def __init__(self, name):
        self.name = name
        self.w = None
        self.r = {}
        self.sem = None
        self.cnt = 0


class Eng:
    def __init__(self, key, h):
        self.key = key
        self.h = h
        self.sem = None
        self.cnt = 0
        self.seen = {}


class TT:
    def __init__(self, t, name):
        self.t = t
        self.buf = Buf(name)


def _flat(items):
    out = []
    for it in items:
        if isinstance(it, TT):
            if it.buf is None:
                out.extend(it.b)
            else:
                out.append(it.buf)
        elif isinstance(it, (list, tuple)):
            out.extend(_flat(it))
        else:
            out.append(it)
    return out


class Ring:
    def __init__(self, items):
        self.items = items
        self.i = 0

    def next(self):
        it = self.items[self.i % len(self.items)]
        self.i += 1
        return it


class K:
    def __init__(self):
        self.nc = bass.Bass("TRN2", target_bir_lowering=False)
        nc = self.nc
        self.stack = [ExitStack()]
        self.nsem = 0
        self.eng = {}
        self.allsem = {}
        for key, h in (("pe", nc.tensor), ("act", nc.scalar), ("dve", nc.vector), ("pool", nc.gpsimd), ("sp", nc.sync)):
            e = Eng(key, h)
            e.sem = self.new_sem("s_" + key)
            self.eng[key] = e

    def new_sem(self, name):
        self.nsem += 1
        return self.stack[0].enter_context(self.nc.semaphore(f"{name}_{self.nsem}"))

    def push(self):
        self.stack.append(ExitStack())

    def pop(self):
        self.stack.pop().close()

    def sbuf(self, name, shape, dtype):
        t = self.stack[-1].enter_context(self.nc.sbuf_tensor(name, list(shape), dtype))
        return TT(t, name)

    def sbuf_blk(self, name, shape, dtype, nblk=4):
        tt = self.sbuf(name, shape, dtype)
        tt.b = [Buf(f"{name}_b{j}") for j in range(nblk)]
        tt.buf = None
        return tt

    def psum(self, name, shape, dtype):
        t = self.stack[-1].enter_context(self.nc.psum_tensor(name, list(shape), dtype))
        return TT(t, name)

    def _wait(self, e, ev):
        if ev.eng == e.key and (e.key == "pe" or not SELF_SYNC or ev.stream):
            return
        if e.seen.get(ev.sem, 0) >= ev.val:
            return
        e.h.wait_ge(ev.sem, ev.val)
        e.seen[ev.sem] = ev.val

    def _deps(self, e, reads, writes):
        for b in reads:
            if b.w is not None:
                self._wait(e, b.w)
        for b in writes:
            if b.w is not None:
                self._wait(e, b.w)
            for ev in b.r.values():
                self._wait(e, ev)

    def _mark(self, ev, reads, writes):
        self.allsem[ev.sem] = max(self.allsem.get(ev.sem, 0), ev.val)
        for b in writes:
            b.w = ev
            b.r = {}
        for b in reads:
            if b not in writes:
                b.r[ev.sem] = ev

    def emit(self, ek, fn, reads=(), writes=(), stream=False):
        e = self.eng[ek]
        reads = _flat(reads)
        writes = _flat(writes)
        self._deps(e, reads, writes)
        fns = fn if isinstance(fn, (list, tuple)) else [fn]
        ins = None
        for f in fns:
            ins = f(e.h)
        if e.cnt >= 30000:
            e.sem = self.new_sem("s_" + ek)
            e.cnt = 0
        e.cnt += 1
        ins.then_inc(e.sem, 1)
        self._mark(Evt(e.sem, e.cnt, ek, stream), reads, writes)

    def dma(self, qk, pairs, slot, reads=(), writes=()):
        e = self.eng[qk]
        slot = slot.buf if isinstance(slot, TT) else slot
        reads = _flat(reads)
        writes = _flat(writes)
        self._deps(e, reads, writes)
        if slot.sem is None:
            slot.sem = self.new_sem("d_" + slot.name)
        for (o, i) in pairs:
            e.h.dma_start(out=o, in_=i).then_inc(slot.sem, 16)
            slot.cnt += 16
        self._mark(Evt(slot.sem, slot.cnt, None), reads, writes)

    def barrier(self, engines=("pe", "act", "dve", "pool", "sp")):
        for ek in engines:
            e = self.eng[ek]
            for sem, val in self.allsem.items():
                self._wait(e, Evt(sem, val, None))


class WStream:
    def __init__(self, k, depth, plan):
        self.k = k
        self.slots = [k.sbuf(f"wslot{i}", [128, 2048], BF16) for i in range(depth)]
        self.plan = plan
        self.issued = 0
        self.nxt = 0
        self.depth = depth
        self.limit = len(plan)

    def _issue(self):
        i = self.issued
        slot = self.slots[i % self.depth]
        tag, grp, fn = self.plan[i]
        self.k.dma("sp", fn(slot.t), slot, reads=[grp], writes=[slot])
        self.issued += 1

    def prefetch(self):
        while self.issued < min(self.limit, self.nxt + self.depth):
            self._issue()

    def pop(self, tag, base=None):
        i = self.nxt
        assert self.plan[i][0] == tag, (i, self.plan[i][0], tag)
        lim = (i if base is None else base) + self.depth
        while self.issued < min(self.limit, lim):
            self._issue()
        self.nxt += 1
        return self.slots[i % self.depth]


def build(upto=3, debug=False):
    k = K()
    nc = k.nc
    dt = lambda name, shape, dtype, kind: nc.dram_tensor(name, list(shape), dtype, kind=kind).ap()
    x_d = dt("x", [S, D], F32, "ExternalInput")
    w_in1 = dt("ffn1_w_in", [2, D, 2 * DFF], F32, "ExternalInput")
    w_out1 = dt("ffn1_w_out", [2, DFF, D], F32, "ExternalInput")
    w_in2 = dt("ffn2_w_in", [2, D, 2 * DFF], F32, "ExternalInput")
    w_out2 = dt("ffn2_w_out", [2, DFF, D], F32, "ExternalInput")
    lng = dt("ln_g", [6, D], F32, "ExternalInput")
    lnb = dt("ln_b", [6, D], F32, "ExternalInput")
    a_wqkv = dt("a_w_qkv", [D, 3 * D], F32, "ExternalInput")
    a_wo = dt("a_w_o", [D, D], F32, "ExternalInput")
    kv_w = dt("kv_w", [D, 512], F32, "ExternalInput")
    b_wq = dt("b_w_q", [D, D], F32, "ExternalInput")
    b_sinks = dt("b_sinks", [2, 8], F32, "ExternalInput")
    b_wo = dt("b_w_o", [D, D], F32, "ExternalInput")
    dmA = dt("dmA", [8, 128, 3 * 2 * 256], F32, "ExternalInput")
    dmB = dt("dmB", [128, 16 * 256], F32, "ExternalInput")
    skind = "ExternalOutput" if debug else "Internal"
    X1 = dt("X1", [S, D], F32, skind)
    QT = dt("QT", [8, 128, S], BF16, skind)
    KT = dt("KT", [8, 128, S], BF16, skind)
    V = dt("V", [S, D], BF16, skind)
    OT = dt("OT", [8, 128, S], BF16, skind)
    out_d = dt("out", [S, D], F32, "ExternalOutput")
    Wi = [dt(f"Wi{f}", [NFC, 128, 2048], BF16, "Internal") for f in range(4)]
    Wo = [dt(f"Wo{f}", [128, NFC * 1024], BF16, "Internal") for f in range(4)]
    Wqkv = dt("Wqkv", [12, 128, 2048], BF16, "Internal")
    WoA = dt("WoA", [4, 128, 2048], BF16, "Internal")
    WoB = dt("WoB", [4, 128, 2048], BF16, "Internal")
    Wkv = dt("Wkv", [3, 128, 2048], BF16, "Internal")
    Wq = dt("Wq", [4, 128, 2048], BF16, "Internal")
    ffn_src = [(w_in1, w_out1, 0), (w_in2, w_out2, 0), (w_in1, w_out1, 1), (w_in2, w_out2, 1)]

    conv = []
    gdiv = [2, 6, 6, 6]
    gWi = [[Buf(f"gWi{f}_{i}") for i in range((NFC + gdiv[f] - 1) // gdiv[f])] for f in range(4)]
    gWo = [Buf(f"gWo{f}") for f in range(4)]
    gQKV, gWoA, gKV, gQ, gWoB = Buf("gQKV"), Buf("gWoA"), Buf("gKV"), Buf("gQ"), Buf("gWoB")

    def kc_view(ap2d):
        return ap2d.rearrange("p (kc c) -> p kc c", kc=8)

    def conv_ffn(f):
        w_in, w_out, l = ffn_src[f]
        wv = w_in[l].rearrange("(kc p) c -> p kc c", p=128)
        def cin(ch):
            cp, isu = ch // 2, ch % 2
            c0 = isu * DFF + cp * 256
            conv.append((gWi[f][ch // gdiv[f]], [(kc_view(Wi[f][ch]), wv[:, :, c0:c0 + 256])]))
        for fc in range(NFC):
            cin(fc)
        wov = w_out[l].rearrange("(fc p) d -> p fc d", p=128)
        dv = Wo[f].rearrange("p (fc d) -> p fc d", fc=NFC)
        for a in range(0, NFC, 4):
            b = min(NFC, a + 4)
            conv.append((gWo[f], [(dv[:, a:b, :], wov[:, a:b, :])]))

    def conv_cols(dst, w2d, ncol_chunks, c_base, grp):
        wv = w2d.rearrange("(kc p) c -> p kc c", p=128)
        for cb in range(ncol_chunks):
            conv.append((grp, [(kc_view(dst[cb]), wv[:, :, c_base + cb * 256:c_base + (cb + 1) * 256])]))

    def conv_wo(dst, w2d, grp):
        wv = w2d.rearrange("(g p) d -> p g d", p=128)
        for c in range(2):
            for q in range(2):
                conv.append((grp, [(dst[c * 2 + q].rearrange("p (r d) -> p r d", r=4), wv[:, 4 * q:4 * q + 4, c * 512:(c + 1) * 512])]))

    def conv_kv():
        wv = kv_w.rearrange("(kc p) c -> p kc c", p=128)
        for gp in range(2):
            dv = kc_view(Wkv[gp])
            prs = []
            for i in range(2):
                g = 2 * gp + i
                for rep in range(2):
                    prs.append((dv[:, :, i * 128 + rep * 64:i * 128 + rep * 64 + 64], wv[:, :, g * 64:(g + 1) * 64]))
            conv.append((gKV, prs))
        conv.append((gKV, [(kc_view(Wkv[2]), wv[:, :, 256:512])]))

    conv_ffn(0)
    conv_cols(Wqkv, a_wqkv, 12, 0, gQKV)
    n_conv_pre = len(conv)
    if upto >= 3:
        conv_wo(WoA, a_wo, gWoA)
        conv_ffn(1)
        conv_kv()
        conv_ffn(2)
        conv_cols(Wq, b_wq, 4, 0, gQ)
        conv_wo(WoB, b_wo, gWoB)
        conv_ffn(3)
    conv_pos = [0]

    def issue_conv(n):
        while n > 0 and conv_pos[0] < len(conv):
            grp, prs = conv[conv_pos[0]]
            k.dma("pool", prs, grp)
            grp.w = Evt(grp.sem, grp.cnt, None)
            conv_pos[0] += 1
            n -= 1

    plan = []

    def plan_ffn(f, tag):
        for fc in range(NFC):
            plan.append((f"{tag}in{fc}", gWi[f][fc // gdiv[f]], lambda st, f=f, fc=fc: [(st[:, :], Wi[f][fc])]))
        wv = Wo[f].rearrange("p (fc d) -> p fc d", fc=NFC)
        for c in range(2):
            for q in range(6):
                nr = 4 if q < 5 else 2
                plan.append((f"{tag}out{c * 6 + q}", gWo[f],
                             lambda st, q=q, c=c, nr=nr, wv=wv: [(st[:, 0:nr * 512].rearrange("p (r d) -> p r d", r=nr),
                                                                 wv[:, 4 * q:4 * q + nr, c * 512:(c + 1) * 512])]))

    def plan_chunks(dst, n, tag, grp, base=0):
        for i in range(n):
            plan.append((f"{tag}{i}", grp, lambda st, i=i: [(st[:, :], dst[base + i])]))

    for tp in range(0, NT, 2):
        for t in (tp, tp + 1):
            plan_ffn(0, f"p1t{t}f")
        for t in (tp, tp + 1):
            plan_chunks(Wqkv, 12, f"p1t{t}qkv", gQKV)
    n_plan_p1 = len(plan)
    if upto >= 3:
        for tp in range(0, NT, 2):
            pair = (tp, tp + 1)
            for t in pair:
                plan_chunks(WoA, 4, f"p3t{t}woA", gWoA)
            for t in pair:
                plan_ffn(1, f"p3t{t}f2")
            for t in pair:
                plan_chunks(Wkv, 3, f"p3t{t}kv", gKV)
            for t in pair:
                plan_ffn(2, f"p3t{t}f3")
            for t in pair:
                plan_chunks(Wq, 4, f"p3t{t}q", gQ)
                plan_chunks(WoB, 4, f"p3t{t}woB", gWoB)
            for t in pair:
                plan_ffn(3, f"p3t{t}f4")

    ws = WStream(k, WDEPTH, plan)
    ws.limit = n_plan_p1
    ident = k.sbuf("ident", [128, 128], BF16)
    identf = k.sbuf("identf", [128, 128], F32)
    ones = k.sbuf("ones", [128, 64], BF16)
    pairA = k.psum("pairA", [128, 2, 512], F32)
    pairB = k.psum("pairB", [128, 2, 512], F32)
    banks = [TT(pairA.t[:, 0, :], "bank0"), TT(pairA.t[:, 1, :], "bank1"), TT(pairB.t[:, 0, :], "bank2"), TT(pairB.t[:, 1, :], "bank3"),
             k.psum("bank4", [128, 512], F32), k.psum("bank5", [128, 512], F32)]
    spairs = [(pairA.t, [banks[0], banks[1]]), (pairB.t, [banks[2], banks[3]])]
    tbanks = Ring([k.psum(f"tbank{i}", [128, 8, 128], BF16) for i in range(2)])

    k.emit("pool", lambda h: h.memset(identf.t[:], 0.0), writes=[identf])
    k.emit("pool", lambda h: h.affine_select(out=identf.t[:], in_=identf.t[:], pattern=[[-1, 128]],
                                             compare_op=ALU.not_equal, fill=1.0, base=0, channel_multiplier=1),
           reads=[identf], writes=[identf])
    k.emit("dve", lambda h: h.tensor_copy(out=ident.t[:], in_=identf.t[:]), reads=[identf], writes=[ident])
    k.emit("dve", lambda h: h.memset(ones.t[:], 1.0), writes=[ones])
    cst = k.sbuf("cst", [128, 3], F32)
    k.emit("pool", [lambda h: h.memset(cst.t[:, 0:1], 4.0 * LN_EPS), lambda h: h.memset(cst.t[:, 1:2], LN_EPS),
                    lambda h: h.memset(cst.t[:, 2:3], -0.5)], writes=[cst])
    issue_conv(n_conv_pre)

    evac_toggle = [0]

    def evac(out_ap, in_ap, reads, writes, eng=None):
        if eng is None:
            eng = ("act", "dve")[evac_toggle[0] % 2]
            evac_toggle[0] += 1
        if eng == "act":
            k.emit("act", lambda h: h.copy(out=out_ap, in_=in_ap), reads=reads, writes=writes, stream=True)
        else:
            k.emit("dve", lambda h: h.tensor_copy(out=out_ap, in_=in_ap), reads=reads, writes=writes, stream=True)

    def transpose_block(xb, j, xT, eng="act"):
        pb = tbanks.next()
        k.emit("pe", [lambda h, kc=kc: h.transpose(out=pb.t[:, kc, :], in_=xb.t[:, j, kc * 128:(kc + 1) * 128],
                                                   identity=ident.t[:]) for kc in range(8)],
               reads=[xb.b[j], ident], writes=[pb])
        evac(xT.t[:, :, j * 128:(j + 1) * 128], pb.t[:, :, :], [pb], [xT.b[j]], eng=eng)

    lnst_ring = Ring([(k.sbuf(f"ln_stats{i}", [128, 2, 6], F32), k.sbuf(f"ln_mv{i}", [128, 2], F32), k.sbuf(f"ln_ve{i}", [128, 1], F32),
                       k.sbuf(f"ln_rs{i}", [128, 1], F32)) for i in range(2)])
    gbring = Ring([(k.sbuf(f"lng{i}", [128, D], F32), k.sbuf(f"lnb{i}", [128, D], F32)) for i in range(2)])

    def layer_norm(xres, j, eps_i, gb, xb):
        stats, mv, ve, rs = lnst_ring.next()
        g_t, b_t = gb
        xa = xres.t[:, j, :]
        k.emit("dve", [lambda h: h.bn_stats(out=stats.t[:, 0, :], in_=xres.t[:, j, 0:512]),
                       lambda h: h.bn_stats(out=stats.t[:, 1, :], in_=xres.t[:, j, 512:1024])],
               reads=[xres.b[j]], writes=[stats])
        k.emit("dve", lambda h: h.bn_aggr(out=mv.t[:], in_=stats.t[:]), reads=[stats], writes=[mv])
        k.emit("dve", lambda h: h.tensor_scalar(out=ve.t[:, :], in0=mv.t[:, 1:2], scalar1=cst.t[:, eps_i:eps_i + 1], scalar2=None, op0=ALU.add),
               reads=[mv, cst], writes=[ve])
        k.emit("pool", lambda h: h.tensor_tensor(out=rs.t[:, :], in0=ve.t[:, :], in1=cst.t[:, 2:3], op=ALU.pow), reads=[ve, cst], writes=[rs])
        k.emit("dve", lambda h: h.scalar_tensor_tensor(out=xa, in0=xa, scalar=mv.t[:, 0:1], in1=g_t.t[:, :], op0=ALU.subtract, op1=ALU.mult),
               reads=[xres.b[j], mv, g_t], writes=[xres.b[j]], stream=True)
        k.emit("dve", lambda h: h.scalar_tensor_tensor(out=xa, in0=xa, scalar=rs.t[:, 0:1], in1=b_t.t[:, :], op0=ALU.mult, op1=ALU.add),
               reads=[xres.b[j], rs, b_t], writes=[xres.b[j]], stream=True)
        k.emit("act", lambda h: h.copy(out=xb.t[:, j, :], in_=xa), reads=[xres.b[j]], writes=[xb.b[j]], stream=True)

    def load_gb(ln_idx):
        g_t, b_t = gbring.next()
        k.dma("sp", [(g_t.t[:, :], lng[ln_idx:ln_idx + 1, :].to_broadcast([128, D]))], g_t, writes=[g_t])
        k.dma("sp", [(b_t.t[:, :], lnb[ln_idx:ln_idx + 1, :].to_broadcast([128, D]))], b_t, writes=[b_t])
        return (g_t, b_t)

    deferred = []

    def step(n=1):
        while n > 0 and deferred:
            _, fns = deferred.pop(0)
            for f in fns:
                f()
            n -= 1

    def flush():
        step(10 ** 6)

    def flush_tile(ts):
        while any(d[0] is ts for d in deferred):
            step()

    class TS:
        def __init__(self, name, attn=False):
            self.xres = k.sbuf_blk(f"xres{name}", [128, NBLK, D], F32)
            self.xb = k.sbuf_blk(f"xb{name}", [128, NBLK, D], BF16)
            self.xT = k.sbuf_blk(f"xT{name}", [128, 8, T], BF16)
            self.sem = Buf(f"xres{name}_dma")
            if attn:
                self.ksT = k.sbuf(f"ksT{name}", [128, 4, T + 128], BF16)
                self.vs = k.sbuf(f"vs{name}", [128, NBLK + 1, 256], BF16)

    def out_phase(lhs_fn, lhs_reads, nrc, tag, ts, scale, eps_i, gb, want_T=True):
        flush_tile(ts)
        ybank = {0: [banks[2], banks[3], banks[4], banks[5]], 1: [banks[0], banks[1], banks[2], banks[3]]}
        nq = (nrc + 3) // 4
        for c in range(2):
            for q in range(nq):
                slot = ws.pop(f"{tag}{c * nq + q}")
                nr = min(4, nrc - 4 * q)
                sv = slot.t[:, 0:nr * 512].rearrange("p (r d) -> p r d", r=nr)
                for ri in range(nr):
                    rc = 4 * q + ri
                    for j in range(NBLK):
                        bb = ybank[c][j]
                        k.emit("pe", lambda h, rc=rc, ri=ri, j=j, bb=bb: h.matmul(bb.t[:, :], lhsT=lhs_fn(rc, j), rhs=sv[:, ri, :],
                                                                               start=(rc == 0), stop=(rc == nrc - 1)),
                               reads=[slot] + list(lhs_reads(rc)), writes=[bb])
                    last = (q == nq - 1 and ri == nr - 1)
                    if nrc <= 8 and ri % 2 == 1 and not last:
                        step()
                if nrc > 8 and q < nq - 1:
                    step()
            for j in range(NBLK):
                bb = ybank[c][j]
                xa = ts.xres.t[:, j, c * 512:(c + 1) * 512]
                k.emit("dve", lambda h, xa=xa, bb=bb: h.scalar_tensor_tensor(out=xa, in0=xa, scalar=float(scale), in1=bb.t[:, :],
                                                                             op0=ALU.mult, op1=ALU.add),
                       reads=[ts.xres.b[j], bb], writes=[ts.xres.b[j]], stream=True)
            step()
        ln = [lambda j=j: layer_norm(ts.xres, j, eps_i, gb, ts.xb) for j in range(NBLK)]
        tr = [lambda j=j: transpose_block(ts.xb, j, ts.xT) for j in range(NBLK)] if want_T else [lambda: None] * NBLK
        deferred.append((ts, [ln[0]]))
        deferred.append((ts, [ln[1]]))
        deferred.append((ts, [ln[2], tr[0]]))
        deferred.append((ts, [ln[3], tr[1]]))
        deferred.append((ts, [tr[2]]))
        deferred.append((ts, [tr[3]]))

    def ffn(f, tag, ts, aT, sg_ring, gb, want_T=True):
        flush_tile(ts)
        xT = ts.xT
        gu = Ring([(banks[0], banks[1]), (banks[2], banks[3]), (banks[4], banks[5])])
        for fc in range(NFC):
            if fc % 2 == 0:
                base_i = ws.nxt
                slot_g = ws.pop(f"{tag}in{fc}", base=base_i)
                slot_u = ws.pop(f"{tag}in{fc + 1}", base=base_i)
                svg = slot_g.t[:, :].rearrange("p (kc c) -> p kc c", kc=8)
                svu = slot_u.t[:, :].rearrange("p (kc c) -> p kc c", kc=8)
            hc = (fc % 2) * 128
            gb_, ub_ = gu.next()
            k.emit("pe", [lambda h, kc=kc: h.matmul(gb_.t[:, :], lhsT=svg[:, kc, hc:hc + 128], rhs=xT.t[:, kc, :], start=(kc == 0), stop=(kc == 7))
                          for kc in range(8)], reads=[slot_g, xT], writes=[gb_])
            k.emit("pe", [lambda h, kc=kc: h.matmul(ub_.t[:, :], lhsT=svu[:, kc, hc:hc + 128], rhs=xT.t[:, kc, :], start=(kc == 0), stop=(kc == 7))
                          for kc in range(8)], reads=[slot_u, xT], writes=[ub_])
            sg = sg_ring.next()
            k.emit("act", lambda h: h.activation(out=sg.t[:, :], in_=gb_.t[:, :], func=AF.Silu), reads=[gb_], writes=[sg], stream=True)
            k.emit("dve", lambda h, fc=fc: h.tensor_tensor(out=aT.t[:, fc, :], in0=sg.t[:, :], in1=ub_.t[:, :], op=ALU.mult),
                   reads=[sg, ub_], writes=[aT.b[fc]], stream=True)
            if fc % 2 == 1:
                step()
        out_phase(lambda rc, j: aT.t[:, rc, j * 128:(j + 1) * 128], lambda rc: [aT.b[rc]], NFC, f"{tag}out", ts, 2.0 * ALPHA, 0, gb, want_T)

    def wo_phase(tag, src, ts, gb):
        out_phase(lambda rc, j: src.t[:, rc, j * 128:(j + 1) * 128], lambda rc: [src], 8, tag, ts, ALPHA, 1, gb)

    def proj_fm(tag_i, dst_fn, xT, nchunks, pr):
        for cb in range(nchunks):
            slot = ws.pop(tag_i(cb))
            sv = slot.t[:, :].rearrange("p (kc c) -> p kc c", kc=8)
            for i in range(2):
                ps = pr.next()
                k.emit("pe", [lambda h, kc=kc, i=i: h.matmul(ps.t[:, :], lhsT=sv[:, kc, i * 128:(i + 1) * 128], rhs=xT.t[:, kc, :],
                                                        start=(kc == 0), stop=(kc == 7)) for kc in range(8)],
                       reads=[slot, xT], writes=[ps])
                o_ap, o_tt = dst_fn(cb * 2 + i)
                evac(o_ap, ps.t[:, :], [ps], [o_tt], eng="act")

    def proj_tm(tag, dst_fn, xT, pr):
        slot = ws.pop(tag)
        sv = slot.t[:, :].rearrange("p (kc c) -> p kc c", kc=8)
        for j in range(NBLK):
            ps = pr.next()
            k.emit("pe", [lambda h, kc=kc, j=j: h.matmul(ps.t[:, 0:256], lhsT=xT.t[:, kc, j * 128:(j + 1) * 128], rhs=sv[:, kc, :],
                                                    start=(kc == 0), stop=(kc == 7)) for kc in range(8)],
                   reads=[slot, xT.b[j]], writes=[ps])
            o_ap, o_tt = dst_fn(j)
            evac(o_ap, ps.t[:, 0:256], [ps], [o_tt], eng="act")

    k.push()
    tsA, tsB = TS("1a"), TS("1b")
    aT = k.sbuf_blk("aT", [128, NFC, T], BF16, NFC)
    sg_ring = Ring([k.sbuf(f"sg{i}", [128, T], F32) for i in range(3)])
    qk_st = k.sbuf("qk_st", [128, 16, T], BF16)
    v_st = k.sbuf("v_st", [128, NBLK, D], BF16)
    x_v = x_d.rearrange("(n j p) d -> n p j d", p=128, j=NBLK)
    X1_v = X1.rearrange("(n j p) d -> n p j d", p=128, j=NBLK)
    V_v = V.rearrange("(n j p) d -> n p j d", p=128, j=NBLK)
    n_conv_rest = len(conv) - n_conv_pre
    pr = Ring([banks[0], banks[1]])
    for tp in range(0, NT, 2):
        pairs_ = ((tp, tsA), (tp + 1, tsB))
        gb = load_gb(0)
        for t, ts in pairs_:
            flush_tile(ts)
            if tp == 0:
                k.dma("sp", [(ts.xres.t[:, :, :], x_v[t])], ts.sem, writes=[ts.xres])
        for t, ts in pairs_:
            for j in range(NBLK):
                if j % 2 == 0:
                    k.emit("act", lambda h, j=j: h.copy(out=ts.xb.t[:, j, :], in_=ts.xres.t[:, j, :]), reads=[ts.xres.b[j]],
                           writes=[ts.xb.b[j]], stream=True)
                else:
                    k.emit("dve", lambda h, j=j: h.tensor_copy(out=ts.xb.t[:, j, :], in_=ts.xres.t[:, j, :]), reads=[ts.xres.b[j]],
                           writes=[ts.xb.b[j]], stream=True)
        for j in range(NBLK):
            transpose_block(pairs_[0][1].xb, j, pairs_[0][1].xT, eng=("act", "dve")[j % 2])
        ts1 = pairs_[1][1]
        deferred.append((ts1, [lambda: transpose_block(ts1.xb, 0, ts1.xT), lambda: transpose_block(ts1.xb, 1, ts1.xT)]))
        deferred.append((ts1, [lambda: transpose_block(ts1.xb, 2, ts1.xT), lambda: transpose_block(ts1.xb, 3, ts1.xT)]))
        for t, ts in pairs_:
            ffn(0, f"p1t{t}f", ts, aT, sg_ring, gb)
            def st1(t=t, ts=ts):
                k.dma("sp", [(X1_v[t], ts.xres.t[:, :, :])], ts.sem, reads=[ts.xres])
                if t + 2 < NT:
                    k.dma("sp", [(ts.xres.t[:, :, :], x_v[t + 2])], ts.sem, writes=[ts.xres])
            deferred.append((ts, [st1]))
        for t, ts in pairs_:
            flush_tile(ts)
            xT = ts.xT
            for cb in range(8):
                step()
                proj_fm(lambda cb_: f"p1t{t}qkv{cb}", lambda idx, cb=cb: (qk_st.t[:, cb * 2 + idx, :], qk_st), xT, 1, pr)
            k.dma("sp", [(QT[:, :, t * T:(t + 1) * T].rearrange("g p s -> p g s"), qk_st.t[:, 0:8, :]),
                         (KT[:, :, t * T:(t + 1) * T].rearrange("g p s -> p g s"), qk_st.t[:, 8:16, :])], qk_st, reads=[qk_st])
            for vc in range(4):
                proj_tm(f"p1t{t}qkv{8 + vc}", lambda j, vc=vc: (v_st.t[:, j, vc * 256:(vc + 1) * 256], v_st), xT, pr)
            k.dma("sp", [(V_v[t], v_st.t[:, :, :])], v_st, reads=[v_st])
    flush()
    if upto < 2:
        issue_conv(len(conv))
    k.barrier()
    k.pop()

    if upto >= 2:
        k.push()
        qT_r = Ring([k.sbuf(f"qT{i}", [128, S], BF16) for i in range(2)])
        kT_r = Ring([k.sbuf(f"kT{i}", [128, S], BF16) for i in range(2)])
        dm_r = Ring([k.sbuf(f"dmt{i}", [128, 3, 2, 256], F32) for i in range(2)])
        vg_r = Ring([k.sbuf(f"vg{i}", [128, 32, 128], BF16) for i in range(2)])
        acc_r = Ring([k.sbuf(f"acc{i}", [128, 2, S], F32) for i in range(2)])
        norm_q = []
        e_r = Ring([k.sbuf(f"e{i}", [128, 2, 256], F32) for i in range(3)])
        p_r = Ring([(k.sbuf(f"pa{i}", [128, 256], BF16), k.sbuf(f"pb{i}", [128, 256], BF16)) for i in range(8)])
        ost_r = Ring([k.sbuf(f"ost{i}", [128, S], BF16) for i in range(2)])
        s_banks = Ring(spairs)
        o_banks = Ring([banks[4], banks[5]])
        stages = [(g, pi) for g in range(8) for pi in range(3)]
        loaded = {}
        chunk_ctr = [0]

        def load_stage(i):
            if i >= len(stages):
                return
            g, pi = stages[i]
            d = PATTERNS[pi][1]
            nb = 32 // d
            if pi == 0:
                qT, kT, dmt = qT_r.next(), kT_r.next(), dm_r.next()
                k.dma("sp", [(qT.t[:, :], QT[g])], qT, writes=[qT])
                k.dma("sp", [(kT.t[:, :], KT[g])], kT, writes=[kT])
                k.dma("sp", [(dmt.t[:, :, :, :].rearrange("p a b c -> p (a b c)"), dmA[g])], dmt, writes=[dmt])
                loaded[("qkd", g)] = (qT, kT, dmt)
            vg = vg_r.next()
            src = V.rearrange("(n i r) c -> i r n c", i=128, r=d)[:, :, :, g * 128:(g + 1) * 128]
            dst = vg.t[:, :, :].rearrange("p (r n) c -> p r n c", r=d)
            prs = [(dst[:, r, n0:min(nb, n0 + 4)], src[:, r, n0:min(nb, n0 + 4)]) for r in range(d) for n0 in range(0, nb, 4)]
            k.dma("sp", prs, vg, writes=[vg])
            loaded[("v", i)] = vg

        load_stage(0)
        for si, (g, pi) in enumerate(stages):
            load_stage(si + 1)
            qT, kT, dmt = loaded[("qkd", g)]
            vg = loaded[("v", si)]
            if pi == 0:
                acc = acc_r.next()
            d = PATTERNS[pi][1]
            nb = 32 // d
            chunks = [(r, n) for r in range(d) for n in range(nb)]
            ptiles = {}
            pending = []

            def tok(base, cnt):
                return slice(base, base + (cnt - 1) * d + 1, d) if d > 1 else slice(base, base + cnt)

            def emit_pv(r, n):
                p = ptiles[(r, n)]
                b = r * nb + n
                ob = o_banks.next()
                ov = ob.t[:, 0:256].rearrange("p (a q) -> p a q", a=2)
                fns = []
                rd = [vg, ones, p[0], p[1]]
                for (lo, hi, hh) in ((0, 64, 0), (64, 128, 1)):
                    for kind in (0, 1):
                        steps = []
                        if n > 0:
                            steps.append((b - 1, ptiles[(r, n - 1)], 128))
                        steps.append((b, p, 0))
                        for ii, (bb, pt, c0) in enumerate(steps):
                            lhsT = vg.t[:, bb, lo:hi] if kind == 0 else ones.t[:, 0:64]
                            fns.append(lambda h, lhsT=lhsT, pt=pt, c0=c0, ii=ii, ns=len(steps), lo=lo, hi=hi, hh=hh, kind=kind:
                                       h.matmul(ov[lo:hi, kind, :], lhsT=lhsT, rhs=pt[hh].t[:, c0:c0 + 128],
                                                start=(ii == 0), stop=(ii == ns - 1)))
                if n > 0:
                    rd.extend(ptiles[(r, n - 1)])
                k.emit("pe", fns, reads=rd, writes=[ob])
                base = r + d * 128 * n
                av = acc.t[:, :, tok(base, 128)]
                if pi == 0:
                    k.emit("act", lambda h: h.copy(out=av, in_=ov), reads=[ob], writes=[acc], stream=True)
                else:
                    k.emit("dve", lambda h: h.tensor_tensor(out=av, in0=av, in1=ov, op=ALU.add), reads=[ob, acc], writes=[acc], stream=True)

            for (r, n) in chunks:
                ncols = 256 if n < nb - 1 else 128
                kb = r + d * 128 * n
                sv, sb = s_banks.next()
                k.emit("pe", [lambda h: h.matmul(sv[:, 0, 0:ncols], lhsT=kT.t[0:64, tok(kb, 128)], rhs=qT.t[0:64, tok(kb, ncols)],
                                                 start=True, stop=True),
                              lambda h: h.matmul(sv[:, 1, 0:ncols], lhsT=kT.t[64:128, tok(kb, 128)], rhs=qT.t[64:128, tok(kb, ncols)],
                                                 start=True, stop=True)],
                       reads=[kT, qT], writes=sb)
                et = e_r.next()
                k.emit("act", lambda h: h.activation(out=et.t[:, :, 0:ncols], in_=sv[:, :, 0:ncols], func=AF.Exp, scale=0.125),
                       reads=sb, writes=[et], stream=True)
                pt = p_r.next()
                chunk_ctr[0] += 1
                want = (chunk_ctr[0] * (len(conv) - n_conv_pre)) // 700
                n_issue = max(0, min(len(conv), n_conv_pre + want) - conv_pos[0])
                k.emit("dve" if n_issue > 0 else "pool",
                       lambda h: h.tensor_tensor(out=pt[0].t[:, 0:ncols], in0=et.t[:, 0, 0:ncols], in1=dmt.t[:, pi, 0, 0:ncols],
                                                 op=ALU.mult), reads=[et, dmt], writes=[pt[0]], stream=True)
                k.emit("dve", lambda h: h.tensor_tensor(out=pt[1].t[:, 0:ncols], in0=et.t[:, 1, 0:ncols], in1=dmt.t[:, pi, 1, 0:ncols],
                                                        op=ALU.mult), reads=[et, dmt], writes=[pt[1]], stream=True)
                issue_conv(n_issue)
                if norm_q:
                    norm_q.pop(0)()
                ptiles[(r, n)] = pt
                pending.append((r, n))
                if len(pending) > 3:
                    emit_pv(*pending.pop(0))
            while pending:
                emit_pv(*pending.pop(0))
            if pi == 2:
                ost = ost_r.next()
                pw = 256 if g < 7 else 1024
                for c0 in range(0, S, pw):
                    def piece(c0=c0, acc_g=acc, ost=ost, pw=pw):
                        k.emit("act", lambda h: h.activation(out=acc_g.t[:, 1, c0:c0 + pw], in_=acc_g.t[:, 1, c0:c0 + pw], func=AF.Ln),
                               reads=[acc_g], writes=[acc_g])
                        k.emit("act", lambda h: h.activation(out=acc_g.t[:, 1, c0:c0 + pw], in_=acc_g.t[:, 1, c0:c0 + pw], func=AF.Exp,
                                                             scale=-1.0), reads=[acc_g], writes=[acc_g])
                        k.emit("dve", lambda h: h.tensor_tensor(out=ost.t[:, c0:c0 + pw], in0=acc_g.t[:, 0, c0:c0 + pw],
                                                                in1=acc_g.t[:, 1, c0:c0 + pw], op=ALU.mult), reads=[acc_g], writes=[ost])
                    norm_q.append(piece)
                norm_q.append(lambda g=g, ost=ost: k.dma("sp", [(OT[g], ost.t[:, :])], ost, reads=[ost]))
                if g == 7:
                    while norm_q:
                        norm_q.pop(0)()
            if si == len(stages) - 3:
                issue_conv(len(conv))
                ws.limit = len(plan)
                ws.prefetch()
        issue_conv(len(conv))
        ws.limit = len(plan)
        k.barrier()
        k.pop()

    if upto >= 3:
        k.push()
        tsA, tsB = TS("3a", attn=True), TS("3b", attn=True)
        aT = k.sbuf_blk("aT3", [128, NFC, T], BF16, NFC)
        sg_ring = Ring([k.sbuf(f"sg3_{i}", [128, T], F32) for i in range(3)])
        qTb = k.sbuf("qTb", [128, 8, T], BF16)
        oTb = k.sbuf("oTb", [128, 8, T], BF16)
        dmb = k.sbuf("dmb", [128, 16, 256], F32)
        esk = k.sbuf("esk", [128, 8], F32)
        e_r = Ring([k.sbuf(f"e3_{i}", [128, 2, 256], F32) for i in range(3)])
        p_r = Ring([k.sbuf(f"p3_{i}", [128, 2, 256], BF16) for i in range(6)])
        dn_r = Ring([k.sbuf(f"dn3_{i}", [128, 2, 128], F32) for i in range(3)])
        X1_v = X1.rearrange("(n j p) d -> n p j d", p=128, j=NBLK)
        out_v = out_d.rearrange("(n j p) d -> n p j d", p=128, j=NBLK)
        k.dma("sp", [(dmb.t[:, :, :].rearrange("p a c -> p (a c)"), dmB[:, :])], dmb, writes=[dmb])
        k.dma("sp", [(esk.t[0:64, :], b_sinks[0:1, :].to_broadcast([64, 8])),
                     (esk.t[64:128, :], b_sinks[1:2, :].to_broadcast([64, 8]))], esk, writes=[esk])
        k.emit("act", lambda h: h.activation(out=esk.t[:, :], in_=esk.t[:, :], func=AF.Exp), reads=[esk], writes=[esk])
        s_banks = Ring(spairs)
        o_banks = Ring([banks[4], banks[5]])
        pr = Ring([banks[0], banks[1]])

        def attention_b(t, ts):
            ksT, vs = ts.ksT, ts.vs
            items = [(jb, i) for jb in range(NBLK) for i in range(8)]
            pend = []

            def emit_pv3(jb, i, pt, first):
                g = i // 2
                ob = o_banks.next()
                ov = ob.t[:, 0:256].rearrange("p (a q) -> p a q", a=2)
                fns = []
                for (lo, hi, hh) in ((0, 64, 0), (64, 128, 1)):
                    for kind in (0, 1):
                        steps = ([] if first else [(jb, 0)]) + [(jb + 1, 128)]
                        for ii, (vb, c0) in enumerate(steps):
                            lhsT = vs.t[:, vb, g * 64:(g + 1) * 64] if kind == 0 else ones.t[:, 0:64]
                            fns.append(lambda h, lhsT=lhsT, c0=c0, ii=ii, ns=len(steps), lo=lo, hi=hi, hh=hh, kind=kind:
                                       h.matmul(ov[lo:hi, kind, :], lhsT=lhsT, rhs=pt.t[:, hh, c0:c0 + 128],
                                                start=(ii == 0), stop=(ii == ns - 1)))
                k.emit("pe", fns, reads=[vs, ones, pt], writes=[ob])
                dn = dn_r.next()
                k.emit("act", [lambda h: h.activation(out=dn.t[:, 1, :], in_=ov[:, 1, :], func=AF.Ln, bias=esk.t[:, i:i + 1], scale=1.0),
                               lambda h: h.copy(out=dn.t[:, 0, :], in_=ov[:, 0, :])], reads=[ob, esk], writes=[dn])
                k.emit("act", lambda h: h.activation(out=dn.t[:, 1, :], in_=dn.t[:, 1, :], func=AF.Exp, scale=-1.0), reads=[dn], writes=[dn])
                k.emit("dve", lambda h: h.tensor_tensor(out=oTb.t[:, i, jb * 128:(jb + 1) * 128], in0=dn.t[:, 0, :], in1=dn.t[:, 1, :], op=ALU.mult),
                       reads=[dn], writes=[oTb], stream=True)

            for (jb, i) in items:
                first = (t == 0 and jb == 0)
                g = i // 2
                sv2, sb = s_banks.next()
                fns = []
                for (lo, hi, hh) in ((0, 64, 0), (64, 128, 1)):
                    if not first:
                        fns.append(lambda h, lo=lo, hi=hi, hh=hh: h.matmul(sv2[:, hh, 0:128], lhsT=ksT.t[lo:hi, g, jb * 128:(jb + 1) * 128],
                                                                         rhs=qTb.t[lo:hi, i, jb * 128:(jb + 1) * 128], start=True, stop=True))
                    fns.append(lambda h, lo=lo, hi=hi, hh=hh: h.matmul(sv2[:, hh, 128:256], lhsT=ksT.t[lo:hi, g, (jb + 1) * 128:(jb + 2) * 128],
                                                                     rhs=qTb.t[lo:hi, i, jb * 128:(jb + 1) * 128], start=True, stop=True))
                k.emit("pe", fns, reads=[ksT, qTb], writes=sb)
                c0 = 128 if first else 0
                et = e_r.next()
                k.emit("act", lambda h: h.activation(out=et.t[:, :, c0:256], in_=sv2[:, :, c0:256], func=AF.Exp, scale=0.125),
                       reads=sb, writes=[et], stream=True)
                pt = p_r.next()
                k.emit("pool", lambda h: h.tensor_tensor(out=pt.t[:, :, c0:256], in0=et.t[:, :, c0:256], in1=dmb.t[:, 2 * i:2 * i + 2, c0:256],
                                                         op=ALU.mult), reads=[et, dmb], writes=[pt], stream=True)
                pend.append((jb, i, pt, first))
                if len(pend) > 3:
                    emit_pv3(*pend.pop(0))
                if i % 4 == 3:
                    step()
            while pend:
                emit_pv3(*pend.pop(0))

        prev_ts = None
        for tp in range(0, NT, 2):
            pairs_ = ((tp, tsA), (tp + 1, tsB))
            gb = load_gb(1)
            for (t, ts), stg in zip(pairs_, (qTb, oTb)):
                flush_tile(ts)
                if tp == 0:
                    k.dma("sp", [(stg.t[:, :, :], OT[:, :, t * T:(t + 1) * T].rearrange("g p s -> p g s"))], stg, writes=[stg])
                    k.dma("sp", [(ts.xres.t[:, :, :], X1_v[t])], ts.sem, writes=[ts.xres])
                wo_phase(f"p3t{t}woA", stg, ts, gb)
            gb = load_gb(2)
            for t, ts in pairs_:
                ffn(1, f"p3t{t}f2", ts, aT, sg_ring, gb)
            for t, ts in pairs_:
                pts = tsB if ts is tsA else tsA
                flush_tile(ts)
                if t > 0:
                    k.emit("act", lambda h: h.copy(out=ts.ksT.t[:, :, 0:128], in_=pts.ksT.t[:, :, T:T + 128]), reads=[pts.ksT], writes=[ts.ksT])
                    k.emit("act", lambda h: h.copy(out=ts.vs.t[:, 0, :], in_=pts.vs.t[:, NBLK, :]), reads=[pts.vs], writes=[ts.vs])
                proj_fm(lambda cb: f"p3t{t}kv{cb}", lambda idx: (ts.ksT.t[:, idx, 128:128 + T], ts.ksT), ts.xT, 1, pr)
                step(2)
                proj_fm(lambda cb: f"p3t{t}kv1", lambda idx: (ts.ksT.t[:, 2 + idx, 128:128 + T], ts.ksT), ts.xT, 1, pr)
                step(2)
                proj_tm(f"p3t{t}kv2", lambda j: (ts.vs.t[:, 1 + j, :], ts.vs), ts.xT, pr)
                step(1)
            gb = load_gb(3)
            for t, ts in pairs_:
                ffn(2, f"p3t{t}f3", ts, aT, sg_ring, gb)
            gb = load_gb(4)
            for t, ts in pairs_:
                flush_tile(ts)
                for cb in range(4):
                    proj_fm(lambda cb_: f"p3t{t}q{cb}", lambda idx, cb=cb: (qTb.t[:, cb * 2 + idx, :], qTb), ts.xT, 1, pr)
                    step()
                attention_b(t, ts)
                wo_phase(f"p3t{t}woB", oTb, ts, gb)
            if tp + 2 < NT:
                for tn, stg in ((tp + 2, qTb), (tp + 3, oTb)):
                    k.dma("sp", [(stg.t[:, :, :], OT[:, :, tn * T:(tn + 1) * T].rearrange("g p s -> p g s"))], stg, writes=[stg])
            gb = load_gb(5)
            for t, ts in pairs_:
                ffn(3, f"p3t{t}f4", ts, aT, sg_ring, gb, want_T=False)
                def st3(t=t, ts=ts):
                    k.dma("sp", [(out_v[t], ts.xres.t[:, :, :])], ts.sem, reads=[ts.xres])
                    if t + 2 < NT:
                        k.dma("sp", [(ts.xres.t[:, :, :], X1_v[t + 2])], ts.sem, writes=[ts.xres])
                deferred.append((ts, [st3]))
        flush()
        k.pop()
    k.barrier(("sp",))
    return k


def alibi_slopes(n):
    return np.array([2.0 ** (-8.0 * (h + 1) / n) for h in range(n)], dtype=np.float64)


def make_tables():
    sl = alibi_slopes(16)
    kk = np.arange(128)[:, None].astype(np.float64)
    cc = np.arange(256)[None, :].astype(np.float64)
    dist = cc - kk
    valid = (dist >= 0) & (dist <= 128)
    dmA = np.zeros((8, 128, 3, 2, 256), np.float32)
    for g in range(8):
        for pi, (_, d) in enumerate(PATTERNS):
            for hh in range(2):
                v = sl[2 * g + hh] * d
                dmA[g, :, pi, hh, :] = np.where(valid, np.exp(-v * np.where(valid, dist, 0.0)), 0.0)
    distB = np.where(cc < 128, 128 + cc - kk, cc - 128 - kk)
    validB = (distB >= 0) & (distB <= 127)
    dmB = np.zeros((128, 16, 256), np.float32)
    for h in range(16):
        dmB[:, h, :] = np.where(validB, np.exp(-sl[h] * np.where(validB, distB, 0.0)), 0.0)
    return dmA.reshape(8, 128, 1536), dmB.reshape(128, 4096)


def make_in_maps(inputs):
    dmA, dmB = make_tables()
    f = lambda a: np.ascontiguousarray(np.asarray(a, dtype=np.float32))
    shared = {
        "ffn1_w_in": f(inputs["ffn1_w_in"]), "ffn1_w_out": f(inputs["ffn1_w_out"]),
        "ffn2_w_in": f(inputs["ffn2_w_in"]), "ffn2_w_out": f(inputs["ffn2_w_out"]),
        "ln_g": f(inputs["ln_g"]).reshape(6, D), "ln_b": f(inputs["ln_b"]).reshape(6, D),
        "a_w_qkv": f(inputs["a_w_qkv"])[0], "a_w_o": f(inputs["a_w_o"])[0], "kv_w": f(inputs["kv_w"]),
        "b_w_q": f(inputs["b_w_q"])[0], "b_sinks": np.ascontiguousarray(f(inputs["b_sinks"]).reshape(8, 2).T),
        "b_w_o": f(inputs["b_w_o"])[0], "dmA": dmA, "dmB": dmB,
    }
    x = f(inputs["x"])
    return [dict(shared, x=x[c]) for c in range(NCORES)]


def kernel(**inputs):
    k = build(upto=3)
    in_maps = make_in_maps(inputs)
    res = run_bass_kernel_spmd(k.nc, in_maps, core_ids=list(range(NCORES)))
    return np.stack([np.asarray(r["out"], dtype=np.float32) for r in res.results], axis=0)
```

```python
import numpy as np
from contextlib import ExitStack
import concourse.bass as bass
import concourse.mybir as mybir
from concourse.bass_utils import run_bass_kernel_spmd

F32 = mybir.dt.float32
BF16 = mybir.dt.bfloat16
ALU = mybir.AluOpType
AF = mybir.ActivationFunctionType

S = 4096
D = 1024
DFF = 2816
NFC = 22
T = 512
NT = S // T
NBLK = T // 128
NCORES = 8
ALPHA = 4.0 ** 0.25
LN_EPS = 1e-5
PATTERNS = ((128, 1), (512, 4), (2048, 16))
SELF_SYNC = True
WDEPTH = 8


class Evt:
    __slots__ = ("sem", "val", "eng", "stream")

    def __init__(self, sem, val, eng, stream=False):
        self.sem = sem
        self.val = val
        self.eng = eng
        self.stream = stream


class Buf:
    def __init__(self, name):
        self.name = name
        self.w = None
        self.r = {}
        self.sem = None
        self.cnt = 0


class Eng:
    def __init__(self, key, h):
        self.key = key
        self.h = h
        self.sem = None
        self.cnt = 0
        self.seen = {}


class TT:
    def __init__(self, t, name):
        self.t = t
        self.buf = Buf(name)


def _flat(items):
    out = []
    for it in items:
        if isinstance(it, TT):
            if it.buf is None:
                out.extend(it.b)
            else:
                out.append(it.buf)
        elif isinstance(it, (list, tuple)):
            out.extend(_flat(it))
        else:
            out.append(it)
    return out


class Ring:
    def __init__(self, items):
        self.items = items
        self.i = 0

    def next(self):
        it = self.items[self.i % len(self.items)]
        self.i += 1
        return it


class K:
    def __init__(self):
        self.nc = bass.Bass("TRN2", target_bir_lowering=False)
        nc = self.nc
        self.stack = [ExitStack()]
        self.nsem = 0
        self.eng = {}
        self.allsem = {}
        for key, h in (("pe", nc.tensor), ("act", nc.scalar), ("dve", nc.vector), ("pool", nc.gpsimd), ("sp", nc.sync)):
            e = Eng(key, h)
            e.sem = self.new_sem("s_" + key)
            self.eng[key] = e

    def new_sem(self, name):
        self.nsem += 1
        return self.stack[0].enter_context(self.nc.semaphore(f"{name}_{self.nsem}"))

    def push(self):
        self.stack.append(ExitStack())

    def pop(self):
        self.stack.pop().close()

    def sbuf(self, name, shape, dtype):
        t = self.stack[-1].enter_context(self.nc.sbuf_tensor(name, list(shape), dtype))
        return TT(t, name)

    def sbuf_blk(self, name, shape, dtype, nblk=4):
        tt = self.sbuf(name, shape, dtype)
        tt.b = [Buf(f"{name}_b{j}") for j in range(nblk)]
        tt.buf = None
        return tt

    def psum(self, name, shape, dtype):
        t = self.stack[-1].enter_context(self.nc.psum_tensor(name, list(shape), dtype))
        return TT(t, name)

    def _wait(self, e, ev):
        if ev.eng == e.key and (e.key == "pe" or not SELF_SYNC or ev.stream):
            return
        if e.seen.get(ev.sem, 0) >= ev.val:
            return
        e.h.wait_ge(ev.sem, ev.val)
        e.seen[ev.sem] = ev.val

    def _deps(self, e, reads, writes):
        for b in reads:
            if b.w is not None:
                self._wait(e, b.w)
        for b in writes:
            if b.w is not None:
                self._wait(e, b.w)
            for ev in b.r.values():
                self._wait(e, ev)

    def _mark(self, ev, reads, writes):
        self.allsem[ev.sem] = max(self.allsem.get(ev.sem, 0), ev.val)
        for b in writes:
            b.w = ev
            b.r = {}
        for b in reads:
            if b not in writes:
                b.r[ev.sem] = ev

    def emit(self, ek, fn, reads=(), writes=(), stream=False):
        e = self.eng[ek]
        reads = _flat(reads)
        writes = _flat(writes)
        self._deps(e, reads, writes)
        fns = fn if isinstance(fn, (list, tuple)) else [fn]
        ins = None
        for f in fns:
            ins = f(e.h)
        if e.cnt >= 30000:
            e.sem = self.new_sem("s_" + ek)
            e.cnt = 0
        e.cnt += 1
        ins.then_inc(e.sem, 1)
        self._mark(Evt(e.sem, e.cnt, ek, stream), reads, writes)

    def dma(self, qk, pairs, slot, reads=(), writes=()):
        e = self.eng[qk]
        slot = slot.buf if isinstance(slot, TT) else slot
        reads = _flat(reads)
        writes = _flat(writes)
        self._deps(e, reads, writes)
        if slot.sem is None:
            slot.sem = self.new_sem("d_" + slot.name)
        for (o, i) in pairs:
            e.h.dma_start(out=o, in_=i).then_inc(slot.sem, 16)
            slot.cnt += 16
        self._mark(Evt(slot.sem, slot.cnt, None), reads, writes)

    def barrier(self, engines=("pe", "act", "dve", "pool", "sp")):
        for ek in engines:
            e = self.eng[ek]
            for sem, val in self.allsem.items():
                self._wait(e, Evt(sem, val, None))


class WStream:
    def __init__(self, k, depth, plan):
        self.k = k
        self.slots = [k.sbuf(f"wslot{i}", [128, 2048], BF16) for i in range(depth)]
        self.plan = plan
        self.issued = 0
        self.nxt = 0
        self.depth = depth
        self.limit = len(plan)

    def _issue(self):
        i = self.issued
        slot = self.slots[i % self.depth]
        tag, grp, fn = self.plan[i]
        self.k.dma("sp", fn(slot.t), slot, reads=[grp], writes=[slot])
        self.issued += 1

    def prefetch(self):
        while self.issued < min(self.limit, self.nxt + self.depth):
            self._issue()

    def pop(self, tag, base=None):
        i = self.nxt
        assert self.plan[i][0] == tag, (i, self.plan[i][0], tag)
        lim = (i if base is None else base) + self.depth
        while self.issued < min(self.limit, lim):
            self._issue()
        self.nxt += 1
        return self.slots[i % self.depth]


def build(upto=3, debug=False):
    k = K()
    nc = k.nc
    dt = lambda name, shape, dtype, kind: nc.dram_tensor(name, list(shape), dtype, kind=kind).ap()
    x_d = dt("x", [S, D], F32, "ExternalInput")
    w_in1 = dt("ffn1_w_in", [2, D, 2 * DFF], F32, "ExternalInput")
    w_out1 = dt("ffn1_w_out", [2, DFF, D], F32, "ExternalInput")
    w_in2 = dt("ffn2_w_in", [2, D, 2 * DFF], F32, "ExternalInput")
    w_out2 = dt("ffn2_w_out", [2, DFF, D], F32, "ExternalInput")
    lng = dt("ln_g", [6, D], F32, "ExternalInput")
    lnb = dt("ln_b", [6, D], F32, "ExternalInput")
    a_wqkv = dt("a_w_qkv", [D, 3 * D], F32, "ExternalInput")
    a_wo = dt("a_w_o", [D, D], F32, "ExternalInput")
    kv_w = dt("kv_w", [D, 512], F32, "ExternalInput")
    b_wq = dt("b_w_q", [D, D], F32, "ExternalInput")
    b_sinks = dt("b_sinks", [2, 8], F32, "ExternalInput")
    b_wo = dt("b_w_o", [D, D], F32, "ExternalInput")
    dmA = dt("dmA", [8, 128, 3 * 2 * 256], F32, "ExternalInput")
    dmB = dt("dmB", [128, 16 * 256], F32, "ExternalInput")
    skind = "ExternalOutput" if debug else "Internal"
    X1 = dt("X1", [S, D], F32, skind)
    QT = dt("QT", [8, 128, S], BF16, skind)
    KT = dt("KT", [8, 128, S], BF16, skind)
    V = dt("V", [S, D], BF16, skind)
    OT = dt("OT", [8, 128, S], BF16, skind)
    out_d = dt("out", [S, D], F32, "ExternalOutput")
    Wi = [dt(f"Wi{f}", [NFC, 128, 2048], BF16, "Internal") for f in range(4)]
    Wo = [dt(f"Wo{f}", [128, NFC * 1024], BF16, "Internal") for f in range(4)]
    Wqkv = dt("Wqkv", [12, 128, 2048], BF16, "Internal")
    WoA = dt("WoA", [4, 128, 2048], BF16, "Internal")
    WoB = dt("WoB", [4, 128, 2048], BF16, "Internal")
    Wkv = dt("Wkv", [3, 128, 2048], BF16, "Internal")
    Wq = dt("Wq", [4, 128, 2048], BF16, "Internal")
    ffn_src = [(w_in1, w_out1, 0), (w_in2, w_out2, 0), (w_in1, w_out1, 1), (w_in2, w_out2, 1)]

    conv = []
    gdiv = [2, 6, 6, 6]
    gWi = [[Buf(f"gWi{f}_{i}") for i in range((NFC + gdiv[f] - 1) // gdiv[f])] for f in range(4)]
    gWo = [Buf(f"gWo{f}") for f in range(4)]
    gQKV, gWoA, gKV, gQ, gWoB = Buf("gQKV"), Buf("gWoA"), Buf("gKV"), Buf("gQ"), Buf("gWoB")

    def kc_view(ap2d):
        return ap2d.rearrange("p (kc c) -> p kc c", kc=8)

    def conv_ffn(f):
        w_in, w_out, l = ffn_src[f]
        wv = w_in[l].rearrange("(kc p) c -> p kc c", p=128)
        def cin(ch):
            cp, isu = ch // 2, ch % 2
            c0 = isu * DFF + cp * 256
            conv.append((gWi[f][ch // gdiv[f]], [(kc_view(Wi[f][ch]), wv[:, :, c0:c0 + 256])]))
        for fc in range(NFC):
            cin(fc)
        wov = w_out[l].rearrange("(fc p) d -> p fc d", p=128)
        dv = Wo[f].rearrange("p (fc d) -> p fc d", fc=NFC)
        for a in range(0, NFC, 4):
            b = min(NFC, a + 4)
            conv.append((gWo[f], [(dv[:, a:b, :], wov[:, a:b, :])]))

    def conv_cols(dst, w2d, ncol_chunks, c_base, grp):
        wv = w2d.rearrange("(kc p) c -> p kc c", p=128)
        for cb in range(ncol_chunks):
            conv.append((grp, [(kc_view(dst[cb]), wv[:, :, c_base + cb * 256:c_base + (cb + 1) * 256])]))

    def conv_wo(dst, w2d, grp):
        wv = w2d.rearrange("(g p) d -> p g d", p=128)
        for c in range(2):
            for q in range(2):
                conv.append((grp, [(dst[c * 2 + q].rearrange("p (r d) -> p r d", r=4), wv[:, 4 * q:4 * q + 4, c * 512:(c + 1) * 512])]))

    def conv_kv():
        wv = kv_w.rearrange("(kc p) c -> p kc c", p=128)
        for gp in range(2):
            dv = kc_view(Wkv[gp])
            prs = []
            for i in range(2):
                g = 2 * gp + i
                for rep in range(2):
                    prs.append((dv[:, :, i * 128 + rep * 64:i * 128 + rep * 64 + 64], wv[:, :, g * 64:(g + 1) * 64]))
            conv.append((gKV, prs))
        conv.append((gKV, [(kc_view(Wkv[2]), wv[:, :, 256:512])]))

    conv_ffn(0)
    conv_cols(Wqkv, a_wqkv, 12, 0, gQKV)
    n_conv_pre = len(conv)
    if upto >= 3:
        conv_wo(WoA, a_wo, gWoA)
        conv_ffn(1)
        conv_kv()
        conv_ffn(2)
        conv_cols(Wq, b_wq, 4, 0, gQ)
        conv_wo(WoB, b_wo, gWoB)
        conv_ffn(3)
    conv_pos = [0]

    def issue_conv(n):
        while n > 0 and conv_pos[0] < len(conv):
            grp, prs = conv[conv_pos[0]]
            k.dma("pool", prs, grp)
            grp.w = Evt(grp.sem, grp.cnt, None)
            conv_pos[0] += 1
            n -= 1

    plan = []

    def plan_ffn(f, tag):
        for fc in range(NFC):
            plan.append((f"{tag}in{fc}", gWi[f][fc // gdiv[f]], lambda st, f=f, fc=fc: [(st[:, :], Wi[f][fc])]))
        wv = Wo[f].rearrange("p (fc d) -> p fc d", fc=NFC)
        for c in range(2):
            for q in range(6):
                nr = 4 if q < 5 else 2
                plan.append((f"{tag}out{c * 6 + q}", gWo[f],
                             lambda st, q=q, c=c, nr=nr, wv=wv: [(st[:, 0:nr * 512].rearrange("p (r d) -> p r d", r=nr),
                                                                 wv[:, 4 * q:4 * q + nr, c * 512:(c + 1) * 512])]))

    def plan_chunks(dst, n, tag, grp, base=0):
        for i in range(n):
            plan.append((f"{tag}{i}", grp, lambda st, i=i: [(st[:, :], dst[base + i])]))

    for tp in range(0, NT, 2):
        for t in (tp, tp + 1):
            plan_ffn(0, f"p1t{t}f")
        for t in (tp, tp + 1):
            plan_chunks(Wqkv, 12, f"p1t{t}qkv", gQKV)
    n_plan_p1 = len(plan)
    if upto >= 3:
        for tp in range(0, NT, 2):
            pair = (tp, tp + 1)
            for t in pair:
                plan_chunks(WoA, 4, f"p3t{t}woA", gWoA)
            for t in pair:
                plan_ffn(1, f"p3t{t}f2")
            for t in pair:
                plan_chunks(Wkv, 3, f"p3t{t}kv", gKV)
            for t in pair:
                plan_ffn(2, f"p3t{t}f3")
            for t in pair:
                plan_chunks(Wq, 4, f"p3t{t}q", gQ)
                plan_chunks(WoB, 4, f"p3t{t}woB", gWoB)
            for t in pair:
                plan_ffn(3, f"p3t{t}f4")

    ws = WStream(k, WDEPTH, plan)
    ws.limit = n_plan_p1
    ident = k.sbuf("ident", [128, 128], BF16)
    identf = k.sbuf("identf", [128, 128], F32)
    ones = k.sbuf("ones", [128, 64], BF16)
    pairA = k.psum("pairA", [128, 2, 512], F32)
    pairB = k.psum("pairB", [128, 2, 512], F32)
    banks = [TT(pairA.t[:, 0, :], "bank0"), TT(pairA.t[:, 1, :], "bank1"), TT(pairB.t[:, 0, :], "bank2"), TT(pairB.t[:, 1, :], "bank3"),
             k.psum("bank4", [128, 512], F32), k.psum("bank5", [128, 512], F32)]
    spairs = [(pairA.t, [banks[0], banks[1]]), (pairB.t, [banks[2], banks[3]])]
    tbanks = Ring([k.psum(f"tbank{i}", [128, 8, 128], BF16) for i in range(2)])

    k.emit("pool", lambda h: h.memset(identf.t[:], 0.0), writes=[identf])
    k.emit("pool", lambda h: h.affine_select(out=identf.t[:], in_=identf.t[:], pattern=[[-1, 128]],
                                             compare_op=ALU.not_equal, fill=1.0, base=0, channel_multiplier=1),
           reads=[identf], writes=[identf])
    k.emit("dve", lambda h: h.tensor_copy(out=ident.t[:], in_=identf.t[:]), reads=[identf], writes=[ident])
    k.emit("dve", lambda h: h.memset(ones.t[:], 1.0), writes=[ones])
    cst = k.sbuf("cst", [128, 3], F32)
    k.emit("pool", [lambda h: h.memset(cst.t[:, 0:1], 4.0 * LN_EPS), lambda h: h.memset(cst.t[:, 1:2], LN_EPS),
                    lambda h: h.memset(cst.t[:, 2:3], -0.5)], writes=[cst])
    issue_conv(n_conv_pre)

    evac_toggle = [0]

    def evac(out_ap, in_ap, reads, writes, eng=None):
        if eng is None:
            eng = ("act", "dve")[evac_toggle[0] % 2]
            evac_toggle[0] += 1
        if eng == "act":
            k.emit("act", lambda h: h.copy(out=out_ap, in_=in_ap), reads=reads, writes=writes, stream=True)
        else:
            k.emit("dve", lambda h: h.tensor_copy(out=out_ap, in_=in_ap), reads=reads, writes=writes, stream=True)

    def transpose_block(xb, j, xT, eng="act"):
        pb = tbanks.next()
        k.emit("pe", [lambda h, kc=kc: h.transpose(out=pb.t[:, kc, :], in_=xb.t[:, j, kc * 128:(kc + 1) * 128],
                                                   identity=ident.t[:]) for kc in range(8)],
               reads=[xb.b[j], ident], writes=[pb])
        evac(xT.t[:, :, j * 128:(j + 1) * 128], pb.t[:, :, :], [pb], [xT.b[j]], eng=eng)

    lnst_ring = Ring([(k.sbuf(f"ln_stats{i}", [128, 2, 6], F32), k.sbuf(f"ln_mv{i}", [128, 2], F32), k.sbuf(f"ln_ve{i}", [128, 1], F32),
                       k.sbuf(f"ln_rs{i}", [128, 1], F32)) for i in range(2)])
    gbring = Ring([(k.sbuf(f"lng{i}", [128, D], F32), k.sbuf(f"lnb{i}", [128, D], F32)) for i in range(2)])

    def layer_norm(xres, j, eps_i, gb, xb):
        stats, mv, ve, rs = lnst_ring.next()
        g_t, b_t = gb
        xa = xres.t[:, j, :]
        k.emit("dve", [lambda h: h.bn_stats(out=stats.t[:, 0, :], in_=xres.t[:, j, 0:512]),
                       lambda h: h.bn_stats(out=stats.t[:, 1, :], in_=xres.t[:, j, 512:1024])],
               reads=[xres.b[j]], writes=[stats])
        k.emit("dve", lambda h: h.bn_aggr(out=mv.t[:], in_=stats.t[:]), reads=[stats], writes=[mv])
        k.emit("dve", lambda h: h.tensor_scalar(out=ve.t[:, :], in0=mv.t[:, 1:2], scalar1=cst.t[:, eps_i:eps_i + 1], scalar2=None, op0=ALU.add),
               reads=[mv, cst], writes=[ve])
        k.emit("pool", lambda h: h.tensor_tensor(out=rs.t[:, :], in0=ve.t[:, :], in1=cst.t[:, 2:3], op=ALU.pow), reads=[ve, cst], writes=[rs])
        k.emit("dve", lambda h: h.scalar_tensor_tensor(out=xa, in0=xa, scalar=mv.t[:, 0:1], in1=g_t.t[:, :], op0=ALU.subtract, op1=ALU.mult),
               reads=[xres.b[j], mv, g_t], writes=[xres.b[j]], stream=True)
        k.emit("dve", lambda h: h.scalar_tensor_tensor(out=xa, in0=xa, scalar=rs.t[:, 0:1], in1=b_t.t[:, :], op0=ALU.mult, op1=ALU.add),
               reads=[xres.b[j], rs, b_t], writes=[xres.b[j]], stream=True)
        k.emit("act", lambda h: h.copy(out=xb.t[:, j, :], in_=xa), reads=[xres.b[j]], writes=[xb.b[j]], stream=True)

    def load_gb(ln_idx):
        g_t, b_t = gbring.next()
        k.dma("sp", [(g_t.t[:, :], lng[ln_idx:ln_idx + 1, :].to_broadcast([128, D]))], g_t, writes=[g_t])
        k.dma("sp", [(b_t.t[:, :], lnb[ln_idx:ln_idx + 1, :].to_broadcast([128, D]))], b_t, writes=[b_t])
        return (g_t, b_t)

    deferred = []

    def step(n=1):
        while n > 0 and deferred:
            _, fns = deferred.pop(0)
            for f in fns:
                f()
            n -= 1

    def flush():
        step(10 ** 6)

    def flush_tile(ts):
        while any(d[0] is ts for d in deferred):
            step()

    class TS:
        def __init__(self, name, attn=False):
            self.xres = k.sbuf_blk(f"xres{name}", [128, NBLK, D], F32)
            self.xb = k.sbuf_blk(f"xb{name}", [128, NBLK, D], BF16)
            self.xT = k.sbuf_blk(f"xT{name}", [128, 8, T], BF16)
            self.sem = Buf(f"xres{name}_dma")
            if attn:
                self.ksT = k.sbuf(f"ksT{name}", [128, 4, T + 128], BF16)
                self.vs = k.sbuf(f"vs{name}", [128, NBLK + 1, 256], BF16)

    def out_phase(lhs_fn, lhs_reads, nrc, tag, ts, scale, eps_i, gb, want_T=True):
        flush_tile(ts)
        ybank = {0: [banks[2], banks[3], banks[4], banks[5]], 1: [banks[0], banks[1], banks[2], banks[3]]}
        nq = (nrc + 3) // 4
        for c in range(2):
            for q in range(nq):
                slot = ws.pop(f"{tag}{c * nq + q}")
                nr = min(4, nrc - 4 * q)
                sv = slot.t[:, 0:nr * 512].rearrange("p (r d) -> p r d", r=nr)
                order = ([(ri, j) for ri in range(nr) for j in (0, 1)] + [(ri, j) for ri in range(nr) for j in (2, 3)]) \
                    if (c == 1 and q == 0) else [(ri, j) for ri in range(nr) for j in range(NBLK)]
                for oi, (ri, j) in enumerate(order):
                    rc = 4 * q + ri
                    if True:
                        bb = ybank[c][j]
                        k.emit("pe", lambda h, rc=rc, ri=ri, j=j, bb=bb: h.matmul(bb.t[:, :], lhsT=lhs_fn(rc, j), rhs=sv[:, ri, :],
                                                                               start=(rc == 0), stop=(rc == nrc - 1)),
                               reads=[slot] + list(lhs_reads(rc)), writes=[bb])
                    last = (q == nq - 1 and ri == nr - 1)
                    if nrc <= 8 and ri % 2 == 1 and j == NBLK - 1 and not last:
                        step()
                if nrc > 8 and q < nq - 1:
                    step()
            for j in range(NBLK):
                bb = ybank[c][j]
                xa = ts.xres.t[:, j, c * 512:(c + 1) * 512]
                k.emit("dve", lambda h, xa=xa, bb=bb: h.scalar_tensor_tensor(out=xa, in0=xa, scalar=float(scale), in1=bb.t[:, :],
                                                                             op0=ALU.mult, op1=ALU.add),
                       reads=[ts.xres.b[j], bb], writes=[ts.xres.b[j]], stream=True)
            step()
        ln = [lambda j=j: layer_norm(ts.xres, j, eps_i, gb, ts.xb) for j in range(NBLK)]
        tr = [lambda j=j: transpose_block(ts.xb, j, ts.xT) for j in range(NBLK)] if want_T else [lambda: None] * NBLK
        deferred.append((ts, [ln[0]]))
        deferred.append((ts, [ln[1]]))
        deferred.append((ts, [ln[2], tr[0]]))
        deferred.append((ts, [ln[3], tr[1]]))
        deferred.append((ts, [tr[2]]))
        deferred.append((ts, [tr[3]]))

    def ffn(f, tag, ts, aT, sg_ring, gb, want_T=True):
        flush_tile(ts)
        xT = ts.xT
        gu = Ring([(banks[0], banks[1]), (banks[2], banks[3]), (banks[4], banks[5])])
        for fc in range(NFC):
            if fc % 2 == 0:
                base_i = ws.nxt
                slot_g = ws.pop(f"{tag}in{fc}", base=base_i)
                slot_u = ws.pop(f"{tag}in{fc + 1}", base=base_i)
                svg = slot_g.t[:, :].rearrange("p (kc c) -> p kc c", kc=8)
                svu = slot_u.t[:, :].rearrange("p (kc c) -> p kc c", kc=8)
            hc = (fc % 2) * 128
            gb_, ub_ = gu.next()
            k.emit("pe", [lambda h, kc=kc: h.matmul(gb_.t[:, :], lhsT=svg[:, kc, hc:hc + 128], rhs=xT.t[:, kc, :], start=(kc == 0), stop=(kc == 7))
                          for kc in range(8)], reads=[slot_g, xT], writes=[gb_])
            k.emit("pe", [lambda h, kc=kc: h.matmul(ub_.t[:, :], lhsT=svu[:, kc, hc:hc + 128], rhs=xT.t[:, kc, :], start=(kc == 0), stop=(kc == 7))
                          for kc in range(8)], reads=[slot_u, xT], writes=[ub_])
            sg = sg_ring.next()
            k.emit("act", lambda h: h.activation(out=sg.t[:, :], in_=gb_.t[:, :], func=AF.Silu), reads=[gb_], writes=[sg], stream=True)
            k.emit("dve", lambda h, fc=fc: h.tensor_tensor(out=aT.t[:, fc, :], in0=sg.t[:, :], in1=ub_.t[:, :], op=ALU.mult),
                   reads=[sg, ub_], writes=[aT.b[fc]], stream=True)
            if fc % 2 == 1:
                step()
        out_phase(lambda rc, j: aT.t[:, rc, j * 128:(j + 1) * 128], lambda rc: [aT.b[rc]], NFC, f"{tag}out", ts, 2.0 * ALPHA, 0, gb, want_T)

    def wo_phase(tag, src, ts, gb):
        out_phase(lambda rc, j: src.t[:, rc, j * 128:(j + 1) * 128], lambda rc: [src], 8, tag, ts, ALPHA, 1, gb)

    def proj_fm(tag_i, dst_fn, xT, nchunks, pr):
        for cb in range(nchunks):
            slot = ws.pop(tag_i(cb))
            sv = slot.t[:, :].rearrange("p (kc c) -> p kc c", kc=8)
            for i in range(2):
                ps = pr.next()
                k.emit("pe", [lambda h, kc=kc, i=i: h.matmul(ps.t[:, :], lhsT=sv[:, kc, i * 128:(i + 1) * 128], rhs=xT.t[:, kc, :],
                                                        start=(kc == 0), stop=(kc == 7)) for kc in range(8)],
                       reads=[slot, xT], writes=[ps])
                o_ap, o_tt = dst_fn(cb * 2 + i)
                evac(o_ap, ps.t[:, :], [ps], [o_tt], eng="act")

    def proj_tm(tag, dst_fn, xT, pr):
        slot = ws.pop(tag)
        sv = slot.t[:, :].rearrange("p (kc c) -> p kc c", kc=8)
        for j in range(NBLK):
            ps = pr.next()
            k.emit("pe", [lambda h, kc=kc, j=j: h.matmul(ps.t[:, 0:256], lhsT=xT.t[:, kc, j * 128:(j + 1) * 128], rhs=sv[:, kc, :],
                                                    start=(kc == 0), stop=(kc == 7)) for kc in range(8)],
                   reads=[slot, xT.b[j]], writes=[ps])
            o_ap, o_tt = dst_fn(j)
            evac(o_ap, ps.t[:, 0:256], [ps], [o_tt], eng="act")

    k.push()
    tsA, tsB = TS("1a"), TS("1b")
    aT = k.sbuf_blk("aT", [128, NFC, T], BF16, NFC)
    sg_ring = Ring([k.sbuf(f"sg{i}", [128, T], F32) for i in range(3)])
    qk_st = k.sbuf("qk_st", [128, 16, T], BF16)
    v_st = k.sbuf("v_st", [128, NBLK, D], BF16)
    x_v = x_d.rearrange("(n j p) d -> n p j d", p=128, j=NBLK)
    X1_v = X1.rearrange("(n j p) d -> n p j d", p=128, j=NBLK)
    V_v = V.rearrange("(n j p) d -> n p j d", p=128, j=NBLK)
    n_conv_rest = len(conv) - n_conv_pre
    pr = Ring([banks[0], banks[1]])
    for tp in range(0, NT, 2):
        pairs_ = ((tp, tsA), (tp + 1, tsB))
        gb = load_gb(0)
        for t, ts in pairs_:
            flush_tile(ts)
            if tp == 0:
                k.dma("sp", [(ts.xres.t[:, :, :], x_v[t])], ts.sem, writes=[ts.xres])
        for t, ts in pairs_:
            for j in range(NBLK):
                if j % 2 == 0:
                    k.emit("act", lambda h, j=j: h.copy(out=ts.xb.t[:, j, :], in_=ts.xres.t[:, j, :]), reads=[ts.xres.b[j]],
                           writes=[ts.xb.b[j]], stream=True)
                else:
                    k.emit("dve", lambda h, j=j: h.tensor_copy(out=ts.xb.t[:, j, :], in_=ts.xres.t[:, j, :]), reads=[ts.xres.b[j]],
                           writes=[ts.xb.b[j]], stream=True)
        for j in range(NBLK):
            transpose_block(pairs_[0][1].xb, j, pairs_[0][1].xT, eng=("act", "dve")[j % 2])
        ts1 = pairs_[1][1]
        deferred.append((ts1, [lambda: transpose_block(ts1.xb, 0, ts1.xT), lambda: transpose_block(ts1.xb, 1, ts1.xT)]))
        deferred.append((ts1, [lambda: transpose_block(ts1.xb, 2, ts1.xT), lambda: transpose_block(ts1.xb, 3, ts1.xT)]))
        for t, ts in pairs_:
            ffn(0, f"p1t{t}f", ts, aT, sg_ring, gb)
            def st1(t=t, ts=ts):
                k.dma("sp", [(X1_v[t], ts.xres.t[:, :, :])], ts.sem, reads=[ts.xres])
                if t + 2 < NT:
                    k.dma("sp", [(ts.xres.t[:, :, :], x_v[t + 2])], ts.sem, writes=[ts.xres])
            deferred.append((ts, [st1]))
        for t, ts in pairs_:
            flush_tile(ts)
            xT = ts.xT
            for cb in range(8):
                step()
                proj_fm(lambda cb_: f"p1t{t}qkv{cb}", lambda idx, cb=cb: (qk_st.t[:, cb * 2 + idx, :], qk_st), xT, 1, pr)
            k.dma("sp", [(QT[:, :, t * T:(t + 1) * T].rearrange("g p s -> p g s"), qk_st.t[:, 0:8, :]),
                         (KT[:, :, t * T:(t + 1) * T].rearrange("g p s -> p g s"), qk_st.t[:, 8:16, :])], qk_st, reads=[qk_st])
            for vc in range(4):
                proj_tm(f"p1t{t}qkv{8 + vc}", lambda j, vc=vc: (v_st.t[:, j, vc * 256:(vc + 1) * 256], v_st), xT, pr)
            k.dma("sp", [(V_v[t], v_st.t[:, :, :])], v_st, reads=[v_st])
    flush()
    if upto < 2:
        issue_conv(len(conv))
    k.barrier()
    k.pop()

    if upto >= 2:
        k.push()
        qT_r = Ring([k.sbuf(f"qT{i}", [128, S], BF16) for i in range(2)])
        kT_r = Ring([k.sbuf(f"kT{i}", [128, S], BF16) for i in range(2)])
        dm_r = Ring([k.sbuf(f"dmt{i}", [128, 3, 2, 256], F32) for i in range(2)])
        vg_r = Ring([k.sbuf(f"vg{i}", [128, 32, 128], BF16) for i in range(2)])
        acc_r = Ring([k.sbuf(f"acc{i}", [128, 2, S], F32) for i in range(2)])
        norm_q = []
        e_r = Ring([k.sbuf(f"e{i}", [128, 2, 256], F32) for i in range(3)])
        p_r = Ring([(k.sbuf(f"pa{i}", [128, 256], BF16), k.sbuf(f"pb{i}", [128, 256], BF16)) for i in range(8)])
        ost_r = Ring([k.sbuf(f"ost{i}", [128, S], BF16) for i in range(2)])
        s_banks = Ring(spairs)
        o_banks = Ring([banks[4], banks[5]])
        stages = [(g, pi) for g in range(8) for pi in range(3)]
        loaded = {}
        chunk_ctr = [0]

        def load_stage(i):
            if i >= len(stages):
                return
            g, pi = stages[i]
            d = PATTERNS[pi][1]
            nb = 32 // d
            if pi == 0:
                qT, kT, dmt = qT_r.next(), kT_r.next(), dm_r.next()
                k.dma("sp", [(qT.t[:, :], QT[g])], qT, writes=[qT])
                k.dma("sp", [(kT.t[:, :], KT[g])], kT, writes=[kT])
                k.dma("sp", [(dmt.t[:, :, :, :].rearrange("p a b c -> p (a b c)"), dmA[g])], dmt, writes=[dmt])
                loaded[("qkd", g)] = (qT, kT, dmt)
            vg = vg_r.next()
            src = V.rearrange("(n i r) c -> i r n c", i=128, r=d)[:, :, :, g * 128:(g + 1) * 128]
            dst = vg.t[:, :, :].rearrange("p (r n) c -> p r n c", r=d)
            prs = [(dst[:, r, n0:min(nb, n0 + 4)], src[:, r, n0:min(nb, n0 + 4)]) for r in range(d) for n0 in range(0, nb, 4)]
            k.dma("sp", prs, vg, writes=[vg])
            loaded[("v", i)] = vg

        load_stage(0)
        for si, (g, pi) in enumerate(stages):
            load_stage(si + 1)
            qT, kT, dmt = loaded[("qkd", g)]
            vg = loaded[("v", si)]
            if pi == 0:
                acc = acc_r.next()
            d = PATTERNS[pi][1]
            nb = 32 // d
            chunks = [(r, n) for r in range(d) for n in range(nb)]
            ptiles = {}
            pending = []

            def tok(base, cnt):
                return slice(base, base + (cnt - 1) * d + 1, d) if d > 1 else slice(base, base + cnt)

            def emit_pv(r, n):
                p = ptiles[(r, n)]
                b = r * nb + n
                ob = o_banks.next()
                ov = ob.t[:, 0:256].rearrange("p (a q) -> p a q", a=2)
                fns = []
                rd = [vg, ones, p[0], p[1]]
                for (lo, hi, hh) in ((0, 64, 0), (64, 128, 1)):
                    for kind in (0, 1):
                        steps = []
                        if n > 0:
                            steps.append((b - 1, ptiles[(r, n - 1)], 128))
                        steps.append((b, p, 0))
                        for ii, (bb, pt, c0) in enumerate(steps):
                            lhsT = vg.t[:, bb, lo:hi] if kind == 0 else ones.t[:, 0:64]
                            fns.append(lambda h, lhsT=lhsT, pt=pt, c0=c0, ii=ii, ns=len(steps), lo=lo, hi=hi, hh=hh, kind=kind:
                                       h.matmul(ov[lo:hi, kind, :], lhsT=lhsT, rhs=pt[hh].t[:, c0:c0 + 128],
                                                start=(ii == 0), stop=(ii == ns - 1)))
                if n > 0:
                    rd.extend(ptiles[(r, n - 1)])
                k.emit("pe", fns, reads=rd, writes=[ob])
                base = r + d * 128 * n
                av = acc.t[:, :, tok(base, 128)]
                if pi == 0:
                    k.emit("act", lambda h: h.copy(out=av, in_=ov), reads=[ob], writes=[acc], stream=True)
                else:
                    k.emit("dve", lambda h: h.tensor_tensor(out=av, in0=av, in1=ov, op=ALU.add), reads=[ob, acc], writes=[acc], stream=True)

            for (r, n) in chunks:
                ncols = 256 if n < nb - 1 else 128
                kb = r + d * 128 * n
                sv, sb = s_banks.next()
                k.emit("pe", [lambda h: h.matmul(sv[:, 0, 0:ncols], lhsT=kT.t[0:64, tok(kb, 128)], rhs=qT.t[0:64, tok(kb, ncols)],
                                                 start=True, stop=True),
                              lambda h: h.matmul(sv[:, 1, 0:ncols], lhsT=kT.t[64:128, tok(kb, 128)], rhs=qT.t[64:128, tok(kb, ncols)],
                                                 start=True, stop=True)],
                       reads=[kT, qT], writes=sb)
                et = e_r.next()
                k.emit("act", lambda h: h.activation(out=et.t[:, :, 0:ncols], in_=sv[:, :, 0:ncols], func=AF.Exp, scale=0.125),
                       reads=sb, writes=[et], stream=True)
                pt = p_r.next()
                chunk_ctr[0] += 1
                want = (chunk_ctr[0] * (len(conv) - n_conv_pre)) // 700
                n_issue = max(0, min(len(conv), n_conv_pre + want) - conv_pos[0])
                k.emit("dve" if n_issue > 0 else "pool",
                       lambda h: h.tensor_tensor(out=pt[0].t[:, 0:ncols], in0=et.t[:, 0, 0:ncols], in1=dmt.t[:, pi, 0, 0:ncols],
                                                 op=ALU.mult), reads=[et, dmt], writes=[pt[0]], stream=True)
                k.emit("dve", lambda h: h.tensor_tensor(out=pt[1].t[:, 0:ncols], in0=et.t[:, 1, 0:ncols], in1=dmt.t[:, pi, 1, 0:ncols],
                                                        op=ALU.mult), reads=[et, dmt], writes=[pt[1]], stream=True)
                issue_conv(n_issue)
                if norm_q:
                    norm_q.pop(0)()
                ptiles[(r, n)] = pt
                pending.append((r, n))
                if len(pending) > 3:
                    emit_pv(*pending.pop(0))
            while pending:
                emit_pv(*pending.pop(0))
            if pi == 2:
                ost = ost_r.next()
                pw = 256 if g < 7 else 1024
                for c0 in range(0, S, pw):
                    def piece(c0=c0, acc_g=acc, ost=ost, pw=pw):
                        k.emit("act", lambda h: h.activation(out=acc_g.t[:, 1, c0:c0 + pw], in_=acc_g.t[:, 1, c0:c0 + pw], func=AF.Ln),
                               reads=[acc_g], writes=[acc_g])
                        k.emit("act", lambda h: h.activation(out=acc_g.t[:, 1, c0:c0 + pw], in_=acc_g.t[:, 1, c0:c0 + pw], func=AF.Exp,
                                                             scale=-1.0), reads=[acc_g], writes=[acc_g])
                        k.emit("dve", lambda h: h.tensor_tensor(out=ost.t[:, c0:c0 + pw], in0=acc_g.t[:, 0, c0:c0 + pw],
                                                                in1=acc_g.t[:, 1, c0:c0 + pw], op=ALU.mult), reads=[acc_g], writes=[ost])
                    norm_q.append(piece)
                norm_q.append(lambda g=g, ost=ost: k.dma("sp", [(OT[g], ost.t[:, :])], ost, reads=[ost]))
                if g == 7:
                    while norm_q:
                        norm_q.pop(0)()
            if si == len(stages) - 3:
                issue_conv(len(conv))
                ws.limit = len(plan)
                ws.prefetch()
        issue_conv(len(conv))
        ws.limit = len(plan)
        k.barrier()
        k.pop()

    if upto >= 3:
        k.push()
        tsA, tsB = TS("3a", attn=True), TS("3b", attn=True)
        aT = k.sbuf_blk("aT3", [128, NFC, T], BF16, NFC)
        sg_ring = Ring([k.sbuf(f"sg3_{i}", [128, T], F32) for i in range(3)])
        qTb = k.sbuf("qTb", [128, 8, T], BF16)
        oTb = k.sbuf("oTb", [128, 8, T], BF16)
        dmb = k.sbuf("dmb", [128, 16, 256], F32)
        esk = k.sbuf("esk", [128, 8], F32)
        e_r = Ring([k.sbuf(f"e3_{i}", [128, 2, 256], F32) for i in range(3)])
        p_r = Ring([k.sbuf(f"p3_{i}", [128, 2, 256], BF16) for i in range(6)])
        dn_r = Ring([k.sbuf(f"dn3_{i}", [128, 2, 128], F32) for i in range(3)])
        X1_v = X1.rearrange("(n j p) d -> n p j d", p=128, j=NBLK)
        out_v = out_d.rearrange("(n j p) d -> n p j d", p=128, j=NBLK)
        k.dma("sp", [(dmb.t[:, :, :].rearrange("p a c -> p (a c)"), dmB[:, :])], dmb, writes=[dmb])
        k.dma("sp", [(esk.t[0:64, :], b_sinks[0:1, :].to_broadcast([64, 8])),
                     (esk.t[64:128, :], b_sinks[1:2, :].to_broadcast([64, 8]))], esk, writes=[esk])
        k.emit("act", lambda h: h.activation(out=esk.t[:, :], in_=esk.t[:, :], func=AF.Exp), reads=[esk], writes=[esk])
        s_banks = Ring(spairs)
        o_banks = Ring([banks[4], banks[5]])
        pr = Ring([banks[0], banks[1]])

        def attention_b(t, ts):
            ksT, vs = ts.ksT, ts.vs
            items = [(jb, i) for jb in range(NBLK) for i in range(8)]
            pend = []

            def emit_pv3(jb, i, pt, first):
                g = i // 2
                ob = o_banks.next()
                ov = ob.t[:, 0:256].rearrange("p (a q) -> p a q", a=2)
                fns = []
                for (lo, hi, hh) in ((0, 64, 0), (64, 128, 1)):
                    for kind in (0, 1):
                        steps = ([] if first else [(jb, 0)]) + [(jb + 1, 128)]
                        for ii, (vb, c0) in enumerate(steps):
                            lhsT = vs.t[:, vb, g * 64:(g + 1) * 64] if kind == 0 else ones.t[:, 0:64]
                            fns.append(lambda h, lhsT=lhsT, c0=c0, ii=ii, ns=len(steps), lo=lo, hi=hi, hh=hh, kind=kind:
                                       h.matmul(ov[lo:hi, kind, :], lhsT=lhsT, rhs=pt.t[:, hh, c0:c0 + 128],
                                                start=(ii == 0), stop=(ii == ns - 1)))
                k.emit("pe", fns, reads=[vs, ones, pt], writes=[ob])
                dn = dn_r.next()
                k.emit("act", [lambda h: h.activation(out=dn.t[:, 1, :], in_=ov[:, 1, :], func=AF.Ln, bias=esk.t[:, i:i + 1], scale=1.0),
                               lambda h: h.copy(out=dn.t[:, 0, :], in_=ov[:, 0, :])], reads=[ob, esk], writes=[dn])
                k.emit("act", lambda h: h.activation(out=dn.t[:, 1, :], in_=dn.t[:, 1, :], func=AF.Exp, scale=-1.0), reads=[dn], writes=[dn])
                k.emit("dve", lambda h: h.tensor_tensor(out=oTb.t[:, i, jb * 128:(jb + 1) * 128], in0=dn.t[:, 0, :], in1=dn.t[:, 1, :], op=ALU.mult),
                       reads=[dn], writes=[oTb], stream=True)

            for (jb, i) in items:
                first = (t == 0 and jb == 0)
                g = i // 2
                sv2, sb = s_banks.next()
                fns = []
                for (lo, hi, hh) in ((0, 64, 0), (64, 128, 1)):
                    if not first:
                        fns.append(lambda h, lo=lo, hi=hi, hh=hh: h.matmul(sv2[:, hh, 0:128], lhsT=ksT.t[lo:hi, g, jb * 128:(jb + 1) * 128],
                                                                         rhs=qTb.t[lo:hi, i, jb * 128:(jb + 1) * 128], start=True, stop=True))
                    fns.append(lambda h, lo=lo, hi=hi, hh=hh: h.matmul(sv2[:, hh, 128:256], lhsT=ksT.t[lo:hi, g, (jb + 1) * 128:(jb + 2) * 128],
                                                                     rhs=qTb.t[lo:hi, i, jb * 128:(jb + 1) * 128], start=True, stop=True))
                k.emit("pe", fns, reads=[ksT, qTb], writes=sb)
                c0 = 128 if first else 0
                et = e_r.next()
                k.emit("act", lambda h: h.activation(out=et.t[:, :, c0:256], in_=sv2[:, :, c0:256], func=AF.Exp, scale=0.125),
                       reads=sb, writes=[et], stream=True)
                pt = p_r.next()
                k.emit("pool", lambda h: h.tensor_tensor(out=pt.t[:, :, c0:256], in0=et.t[:, :, c0:256], in1=dmb.t[:, 2 * i:2 * i + 2, c0:256],
                                                         op=ALU.mult), reads=[et, dmb], writes=[pt], stream=True)
                pend.append((jb, i, pt, first))
                if len(pend) > 3:
                    emit_pv3(*pend.pop(0))
                if i % 4 == 3:
                    step()
            while pend:
                emit_pv3(*pend.pop(0))

        prev_ts = None
        for tp in range(0, NT, 2):
            pairs_ = ((tp, tsA), (tp + 1, tsB))
            gb = load_gb(1)
            for (t, ts), stg in zip(pairs_, (qTb, oTb)):
                flush_tile(ts)
                if tp == 0:
                    k.dma("sp", [(stg.t[:, :, :], OT[:, :, t * T:(t + 1) * T].rearrange("g p s -> p g s"))], stg, writes=[stg])
                    k.dma("sp", [(ts.xres.t[:, :, :], X1_v[t])], ts.sem, writes=[ts.xres])
                wo_phase(f"p3t{t}woA", stg, ts, gb)
            gb = load_gb(2)
            for t, ts in pairs_:
                ffn(1, f"p3t{t}f2", ts, aT, sg_ring, gb)
            for t, ts in pairs_:
                pts = tsB if ts is tsA else tsA
                flush_tile(ts)
                if t > 0:
                    k.emit("act", lambda h: h.copy(out=ts.ksT.t[:, :, 0:128], in_=pts.ksT.t[:, :, T:T + 128]), reads=[pts.ksT], writes=[ts.ksT])
                    k.emit("act", lambda h: h.copy(out=ts.vs.t[:, 0, :], in_=pts.vs.t[:, NBLK, :]), reads=[pts.vs], writes=[ts.vs])
                proj_fm(lambda cb: f"p3t{t}kv{cb}", lambda idx: (ts.ksT.t[:, idx, 128:128 + T], ts.ksT), ts.xT, 1, pr)
                step(2)
                proj_fm(lambda cb: f"p3t{t}kv1", lambda idx: (ts.ksT.t[:, 2 + idx, 128:128 + T], ts.ksT), ts.xT, 1, pr)
                step(2)
                proj_tm(f"p3t{t}kv2", lambda j: (ts.vs.t[:, 1 + j, :], ts.vs), ts.xT, pr)
                step(1)
            gb = load_gb(3)
            for t, ts in pairs_:
                ffn(2, f"p3t{t}f3", ts, aT, sg_ring, gb)
            gb = load_gb(4)
            for t, ts in pairs_:
                flush_tile(ts)
                for cb in range(4):
                    proj_fm(lambda cb_: f"p3t{t}q{cb}", lambda idx, cb=cb: (qTb.t[:, cb * 2 + idx, :], qTb), ts.xT, 1, pr)
                    step()
                attention_b(t, ts)
                wo_phase(f"p3t{t}woB", oTb, ts, gb)
            if tp + 2 < NT:
                for tn, stg in ((tp + 2, qTb), (tp + 3, oTb)):
                    k.dma("sp", [(stg.t[:, :, :], OT[:, :, tn * T:(tn + 1) * T].rearrange("g p s -> p g s"))], stg, writes=[stg])
            gb = load_gb(5)
            for t, ts in pairs_:
                ffn(3, f"p3t{t}f4", ts, aT, sg_ring, gb, want_T=False)
                def st3(t=t, ts=ts):
                    k.dma("sp", [(out_v[t], ts.xres.t[:, :, :])], ts.sem, reads=[ts.xres])
                    if t + 2 < NT:
                        k.dma("sp", [(ts.xres.t[:, :, :], X1_v[t + 2])], ts.sem, writes=[ts.xres])
                deferred.append((ts, [st3]))
        flush()
        k.pop()
    k.barrier(("sp",))
    return k


def alibi_slopes(n):
    return np.array([2.0 ** (-8.0 * (h + 1) / n) for h in range(n)], dtype=np.float64)


def make_tables():
    sl = alibi_slopes(16)
    kk = np.arange(128)[:, None].astype(np.float64)
    cc = np.arange(256)[None, :].astype(np.float64)
    dist = cc - kk
    valid = (dist >= 0) & (dist <= 128)
    dmA = np.zeros((8, 128, 3, 2, 256), np.float32)
    for g in range(8):
        for pi, (_, d) in enumerate(PATTERNS):
            for hh in range(2):
                v = sl[2 * g + hh] * d
                dmA[g, :, pi, hh, :] = np.where(valid, np.exp(-v * np.where(valid, dist, 0.0)), 0.0)
    distB = np.where(cc < 128, 128 + cc - kk, cc - 128 - kk)
    validB = (distB >= 0) & (distB <= 127)
    dmB = np.zeros((128, 16, 256), np.float32)
    for h in range(16):
        dmB[:, h, :] = np.where(validB, np.exp(-sl[h] * np.where(validB, distB, 0.0)), 0.0)
    return dmA.reshape(8, 128, 1536), dmB.reshape(128, 4096)


def make_in_maps(inputs):
    dmA, dmB = make_tables()
    f = lambda a: np.ascontiguousarray(np.asarray(a, dtype=np.float32))
    shared = {
        "ffn1_w_in": f(inputs["ffn1_w_in"]), "ffn1_w_out": f(inputs["ffn1_w_out"]),
        "ffn2_w_in": f(inputs["ffn2_w_in"]), "ffn2_w_out": f(inputs["ffn2_w_out"]),
        "ln_g": f(inputs["ln_g"]).reshape(6, D), "ln_b": f(inputs["ln_b"]).reshape(6, D),
        "a_w_qkv": f(inputs["a_w_qkv"])[0], "a_w_o": f(inputs["a_w_o"])[0], "kv_w": f(inputs["kv_w"]),
        "b_w_q": f(inputs["b_w_q"])[0], "b_sinks": np.ascontiguousarray(f(inputs["b_sinks"]).reshape(8, 2).T),
        "b_w_o": f(inputs["b_w_o"])[0], "dmA": dmA, "dmB": dmB,
    }
    x = f(inputs["x"])
    return [dict(shared, x=x[c]) for c in range(NCORES)]


def kernel(**inputs):
    k = build(upto=3)
    in_maps = make_in_maps(inputs)
    res = run_bass_kernel_spmd(k.nc, in_maps, core_ids=list(range(NCORES)))
    return np.stack([np.asarray(r["out"], dtype=np.float32) for r in res.results], axis=0)
```
